# Optimizing a Trainium2 kernel written in Bass

```python
import jax, jax.numpy as jnp
from jax import lax
import numpy as np

D_MODEL = 1024
BATCH = 8
SEQ = 4096
DEPTH = 4

CHUNK = 64
N_MEM = 256
NORM_EPS = 1e-6
NEG_BIG = -1e30
GATE_CLIP = 60.0
MIX_WIDTH = D_MODEL
POOL_WIDTH = MIX_WIDTH // 2
POOL_WINDOWS = (2, 4, 8, 16)
POOL_GROUPS = len(POOL_WINDOWS)
POOL_GROUP_DIM = POOL_WIDTH // POOL_GROUPS
REC_WIDTH = MIX_WIDTH - POOL_WIDTH
REC_EXPAND = 128
REC_HEADS = REC_WIDTH // REC_EXPAND
REC_HEAD_DIM = REC_WIDTH // REC_HEADS
ATT_HEADS = 16
ATT_HEAD_DIM = MIX_WIDTH // ATT_HEADS
BAND_CHUNKS = 8
BAND = (BAND_CHUNKS + 1) * CHUNK
REL_CLIP = 256
MEM_HEADS = 4
MEM_HEAD_DIM = D_MODEL // MEM_HEADS
FF_DIM = 4 * D_MODEL
N_EVEN = (DEPTH + 1) // 2
N_ODD = DEPTH // 2

kernel_name = 'hybrid_pool_hgrn2_chunkattn_trunk'


def rmsnorm(x, g):
    xf = x.astype(jnp.float32)
    y = xf * lax.rsqrt(jnp.mean(xf * xf, axis=-1, keepdims=True) + NORM_EPS)
    return y.astype(x.dtype) * g


def causal_multiscale_pool(u):
    S_ = u.shape[1]
    uf = u.astype(jnp.float32)
    cs = jnp.cumsum(uf, axis=1)
    pos = jnp.arange(1, S_ + 1, dtype=jnp.float32)
    outs = []
    for g, w in enumerate(POOL_WINDOWS):
        c = cs[:, :, g]
        lag = jnp.pad(c, ((0, 0), (w, 0), (0, 0)))[:, :S_]
        outs.append((c - lag) / jnp.minimum(pos, w)[:, None])
    mean = jnp.stack(outs, axis=2)
    return (mean - uf).astype(u.dtype)


def hgrn2_chunkwise(q, k, v, logf):
    B_, S_, H, dk = q.shape
    dv = v.shape[-1]
    N = S_ // CHUNK

    def to_chunks(a):
        return a.reshape(B_, N, CHUNK, H, a.shape[-1]).transpose(1, 0, 3, 2, 4)

    causal = jnp.tril(jnp.ones((CHUNK, CHUNK), dtype=bool))[None, None, :, :, None]

    def step(state, inp):
        qc, kc, vc, lc = inp
        b = jnp.cumsum(lc, axis=2)
        o_inter = jnp.einsum('bhtd,bhde->bhte', qc * jnp.exp(b), state)
        diff = b[:, :, :, None, :] - b[:, :, None, :, :]
        decay = jnp.where(causal, jnp.exp(jnp.where(causal, diff, 0.0)), 0.0)
        scores = jnp.einsum('bhtd,bhsd,bhtsd->bhts', qc, kc, decay)
        o = o_inter + jnp.einsum('bhts,bhse->bhte', scores, vc)
        b_last = b[:, :, -1:, :]
        new_state = jnp.exp(b_last[:, :, 0, :])[..., None] * state + jnp.einsum(
            'bhsd,bhse->bhde', kc * jnp.exp(b_last - b), vc)
        return new_state, o

    init = jnp.zeros((B_, H, dk, dv), jnp.float32)
    _, o = lax.scan(step, init, (to_chunks(q), to_chunks(k), to_chunks(v), to_chunks(logf)))
    return o.transpose(1, 0, 3, 2, 4).reshape(B_, S_, H, dv)


def even_mixer(h, w_in, pool_w, pool_scale, lb, out_gain, w_out):
    B_, S_, _ = h.shape
    z = h @ w_in
    cuts = [POOL_WIDTH + r * REC_WIDTH for r in range(4)]
    u, q, fz, i, g = jnp.split(z, cuts, axis=-1)
    pooled = causal_multiscale_pool(u.reshape(B_, S_, POOL_GROUPS, POOL_GROUP_DIM))
    a_out = jnp.einsum('bsgc,gcd->bsgd', pooled, pool_w).reshape(B_, S_, POOL_WIDTH) * pool_scale
    fz32 = jnp.clip(fz.astype(jnp.float32), -GATE_CLIP, GATE_CLIP)
    logf = jax.nn.log_sigmoid(fz32) + jnp.log1p(lb * jnp.exp(-fz32))
    k = (1.0 - lb) * jax.nn.sigmoid(-fz32)
    hs = lambda a: a.reshape(B_, S_, REC_HEADS, -1)
    o = hgrn2_chunkwise(hs(q.astype(jnp.float32)), hs(k), hs(i.astype(jnp.float32)), hs(logf))
    o = rmsnorm(o, out_gain.astype(jnp.float32)) * jax.nn.silu(hs(g.astype(jnp.float32)))
    b_out = o.reshape(B_, S_, REC_WIDTH).astype(h.dtype)
    return jnp.concatenate([a_out.astype(h.dtype), b_out], axis=-1) @ w_out


def chunk_band_attention(q, k, v, rel_bias):
    B_, S_, H, dh = q.shape
    N = S_ // CHUNK
    pad = ((0, 0), (BAND_CHUNKS * CHUNK, 0), (0, 0), (0, 0))
    kp = jnp.pad(k, pad)
    vp = jnp.pad(v, pad)
    t = jnp.arange(CHUNK)[:, None]
    j = jnp.arange(BAND)[None, :]
    rel = BAND_CHUNKS * CHUNK + t - j
    idx = jnp.clip(rel, -REL_CLIP, REL_CLIP) + REL_CLIP
    bias = rel_bias[:, idx].astype(jnp.float32)
    scale = dh ** -0.5

    def per_chunk(n):
        qn = lax.dynamic_slice_in_dim(q, n * CHUNK, CHUNK, axis=1)
        kn = lax.dynamic_slice_in_dim(kp, n * CHUNK, BAND, axis=1)
        vn = lax.dynamic_slice_in_dim(vp, n * CHUNK, BAND, axis=1)
        s = jnp.einsum('bthd,bjhd->bhtj', qn, kn).astype(jnp.float32) * scale + bias
        valid = (n - BAND_CHUNKS) * CHUNK + jnp.arange(BAND) >= 0
        s = jnp.where(valid[None, None, None, :], s, NEG_BIG)
        p = jax.nn.softmax(s, axis=-1).astype(v.dtype)
        return jnp.einsum('bhtj,bjhd->bthd', p, vn)

    out = lax.map(per_chunk, jnp.arange(N))
    return jnp.moveaxis(out, 0, 1).reshape(B_, S_, H, dh)


def odd_mixer(h, w_qkv, rel_bias, w_o):
    B_, S_, _ = h.shape
    qkv = (h @ w_qkv).reshape(B_, S_, 3, ATT_HEADS, ATT_HEAD_DIM)
    o = chunk_band_attention(qkv[:, :, 0], qkv[:, :, 1], qkv[:, :, 2], rel_bias)
    return o.reshape(B_, S_, MIX_WIDTH) @ w_o


def memory_cross_attention(h, mem_n, w_q, w_kv, w_o):
    B_, S_, _ = h.shape
    M = mem_n.shape[1]
    q = (h @ w_q).reshape(B_, S_, MEM_HEADS, MEM_HEAD_DIM)
    kv = (mem_n @ w_kv).reshape(B_, M, 2, MEM_HEADS, MEM_HEAD_DIM)
    s = jnp.einsum('bshd,bmhd->bhsm', q, kv[:, :, 0]).astype(jnp.float32) * MEM_HEAD_DIM ** -0.5
    p = jax.nn.softmax(s, axis=-1).astype(h.dtype)
    o = jnp.einsum('bhsm,bmhd->bshd', p, kv[:, :, 1]).reshape(B_, S_, D_MODEL)
    return o @ w_o


def squared_relu_mlp(h, w1, w2):
    a = jax.nn.relu(h @ w1)
    return (a * a) @ w2


def setup_inputs(seed: int = 0) -> dict:
    key = jax.random.key(seed)
    ks = jax.random.split(key, 20)
    f32 = jnp.float32

    def dense(k, shape):
        return jax.random.normal(k, shape, f32) * shape[-2] ** -0.5

    return {
        'x': jax.random.normal(ks[0], (BATCH, SEQ, D_MODEL), f32),
        'mem': jax.random.normal(ks[1], (BATCH, N_MEM, D_MODEL), f32),
        'norm_gains': 1.0 + 0.05 * jax.random.normal(ks[2], (DEPTH, 6, D_MODEL), f32),
        'mem_norm_gain': 1.0 + 0.05 * jax.random.normal(ks[3], (D_MODEL,), f32),
        'w_in_ab': dense(ks[4], (N_EVEN, D_MODEL, POOL_WIDTH + 4 * REC_WIDTH)),
        'pool_w': dense(ks[5], (N_EVEN, POOL_GROUPS, POOL_GROUP_DIM, POOL_GROUP_DIM)),
        'pool_scale': 1.0 + 0.1 * jax.random.normal(ks[6], (N_EVEN, POOL_WIDTH), f32),
        'rec_lb_logits': 0.1 * jax.random.normal(ks[7], (N_EVEN, REC_WIDTH), f32),
        'rec_out_gain': 1.0 + 0.05 * jax.random.normal(ks[8], (N_EVEN, REC_HEAD_DIM), f32),
        'w_out_ab': dense(ks[9], (N_EVEN, MIX_WIDTH, D_MODEL)),
        'w_qkv': dense(ks[10], (N_ODD, D_MODEL, 3 * MIX_WIDTH)),
        'rel_bias': 0.1 * jax.random.normal(ks[11], (N_ODD, ATT_HEADS, 2 * REL_CLIP + 1), f32),
        'w_o_att': dense(ks[12], (N_ODD, MIX_WIDTH, D_MODEL)),
        'w_mem_q': dense(ks[13], (DEPTH, D_MODEL, D_MODEL)),
        'w_mem_kv': dense(ks[14], (DEPTH, D_MODEL, 2 * D_MODEL)),
        'w_mem_o': dense(ks[15], (DEPTH, D_MODEL, D_MODEL)),
        'w_ff1': dense(ks[16], (DEPTH, D_MODEL, FF_DIM)),
        'w_ff2': dense(ks[17], (DEPTH, FF_DIM, D_MODEL)),
    }


def reference(x, mem, norm_gains, mem_norm_gain, w_in_ab, pool_w, pool_scale, rec_lb_logits,
              rec_out_gain, w_out_ab, w_qkv, rel_bias, w_o_att, w_mem_q, w_mem_kv, w_mem_o,
              w_ff1, w_ff2):
    lb_p = jax.nn.softmax(rec_lb_logits.astype(jnp.float32), axis=0)
    lbs = jnp.clip(jnp.cumsum(lb_p, axis=0) - lb_p[0], 0.0, 1.0)
    mem_n = rmsnorm(mem, mem_norm_gain)
    for l in range(DEPTH):
        g = norm_gains[l]
        h = rmsnorm(x, g[0])
        if l % 2 == 0:
            e = l // 2
            y = even_mixer(h, w_in_ab[e], pool_w[e], pool_scale[e], lbs[e], rec_out_gain[e],
                           w_out_ab[e])
        else:
            o = l // 2
            y = odd_mixer(h, w_qkv[o], rel_bias[o], w_o_att[o])
        x = x + rmsnorm(y, g[1])
        h = rmsnorm(x, g[2])
        x = x + rmsnorm(memory_cross_attention(h, mem_n, w_mem_q[l], w_mem_kv[l], w_mem_o[l]), g[3])
        h = rmsnorm(x, g[4])
        x = x + rmsnorm(squared_relu_mlp(h, w_ff1[l], w_ff2[l]), g[5])
    return x
```

```python
import sys
from contextlib import ExitStack
import numpy as np
import concourse.bass as bass
import concourse.mybir as mybir
from concourse.bass_utils import run_bass_kernel_spmd

F32 = mybir.dt.float32
BF16 = mybir.dt.bfloat16
AF = mybir.ActivationFunctionType
ALU = mybir.AluOpType

D = 1024
T = 512
NW = 4
EPS = 1e-6
ENGS = ("pe", "act", "dve", "pool", "sp")
SUBS = ("mix", "mem", "mlp")
MIXPART = "all"
MIXSTOP = 6


class Buf:
    __slots__ = ("w", "r")

    def __init__(self):
        self.w = None
        self.r = {}


def bufs(n):
    return [Buf() for _ in range(n)]


class Chan:
    __slots__ = ("sem", "count")

    def __init__(self, sem):
        self.sem = sem
        self.count = 0


class Op:
    __slots__ = ("fn", "deps", "signal", "cnt", "dma", "src")

    def __init__(self, fn, deps, dma=None):
        try:
            f = sys._getframe(2)
            self.src = (f.f_lineno, f.f_back.f_lineno if f.f_back else 0, f.f_back.f_back.f_lineno if f.f_back and f.f_back.f_back else 0)
        except ValueError:
            self.src = None
        self.fn = fn
        self.deps = deps
        self.signal = False
        self.cnt = 0
        self.dma = dma


def _flat(x, out):
    if x is None:
        return out
    if isinstance(x, Buf):
        out.append(x)
        return out
    for y in x:
        _flat(y, out)
    return out


class Prog:
    def __init__(self, nc, stack):
        self.nc = nc
        self.ops = {e: [] for e in ENGS}
        self.sem = {e: stack.enter_context(nc.semaphore("s_" + e)) for e in ENGS}
        self.stack = stack
        self.chans = []

    def chan(self):
        c = Chan(self.stack.enter_context(self.nc.semaphore("c%d" % len(self.chans))))
        self.chans.append(c)
        return c

    def _collect(self, eng, rd, wr):
        deps = {}
        for b in rd:
            if b.w is not None:
                deps[b.w[1:]] = b.w
        for b in wr:
            if b.w is not None:
                deps[b.w[1:]] = b.w
            for t in b.r.values():
                deps[t[1:]] = t
        out = []
        for t in deps.values():
            if t[0] == "e":
                if t[1] == eng and eng == "pe":
                    continue
                self.ops[t[1]][t[2]].signal = True
            out.append(t)
        return out

    def _mark(self, tok, rd, wr):
        key = tok[1]
        for b in rd:
            b.r[key] = tok
        for b in wr:
            b.w = tok
            b.r = {}

    def op(self, eng, fn, reads=None, writes=None):
        rd = _flat(reads, [])
        wr = _flat(writes, [])
        deps = self._collect(eng, rd, wr)
        seq = len(self.ops[eng])
        self.ops[eng].append(Op(fn, deps))
        self._mark(("e", eng, seq), rd, wr)

    def dma(self, q, chan, out_ap, in_ap, reads=None, writes=None):
        rd = _flat(reads, [])
        wr = _flat(writes, [])
        deps = self._collect(q, rd, wr)
        chan.count += 16
        tok = ("d", chan, chan.count)
        self.ops[q].append(Op(lambda e: e.dma_start(out=out_ap, in_=in_ap), deps, dma=(chan, chan.count)))
        self._mark(tok, rd, wr)
        return tok

    def barrier(self):
        last = {}
        for e in ENGS:
            if self.ops[e]:
                for i in range(len(self.ops[e]) - 1, -1, -1):
                    if self.ops[e][i].fn is not None and self.ops[e][i].dma is None:
                        last[e] = i
                        break
        for e in ENGS:
            deps = []
            for f, i in last.items():
                if f != e:
                    self.ops[f][i].signal = True
                    deps.append(("e", f, i))
            for c in self.chans:
                if c.count:
                    deps.append(("d", c, c.count))
            self.ops[e].append(Op(None, deps))

    def emit(self):
        nc = self.nc
        LIM = 16000
        nsig = {}
        for e in ENGS:
            c = 0
            for o in self.ops[e]:
                if o.signal:
                    c += 1
                o.cnt = c
            nsig[e] = c
        sems = {}
        for e in ENGS:
            sems[e] = [self.sem[e]]
            for i in range(1, (nsig[e] + LIM - 1) // LIM):
                sems[e].append(self.stack.enter_context(nc.semaphore("s_%s_%d" % (e, i))))
        stats = {}
        with nc.Block() as block:
            def run(eng, e):
                seen = {}
                nw = 0
                for o in self.ops[eng]:
                    for t in o.deps:
                        if t[0] == "e":
                            key = t[1]
                            val = self.ops[t[1]][t[2]].cnt
                            if seen.get(key, 0) >= val:
                                continue
                            seen[key] = val
                            e.wait_ge(sems[t[1]][(val - 1) // LIM], (val - 1) % LIM + 1)
                        else:
                            key = id(t[1])
                            val = t[2]
                            if seen.get(key, 0) >= val:
                                continue
                            seen[key] = val
                            e.wait_ge(t[1].sem, val)
                        nw += 1
                    if o.fn is None:
                        continue
                    try:
                        ins = o.fn(e)
                    except Exception:
                        print("FAILED OP", eng, "src lines", o.src, flush=True)
                        raise
                    if o.dma is not None:
                        ins.then_inc(o.dma[0].sem, 16)
                    elif o.signal:
                        ins.then_inc(sems[eng][(o.cnt - 1) // LIM], 1)
                stats[eng] = (len(self.ops[eng]), nw, nsig[eng])

            @block.tensor
            def _(e):
                run("pe", e)

            @block.scalar
            def _(e):
                run("act", e)

            @block.vector
            def _(e):
                run("dve", e)

            @block.gpsimd
            def _(e):
                run("pool", e)

            @block.sync
            def _(e):
                run("sp", e)
        return stats


def build(S, layers):
    NT = S // T
    nc = bass.Bass("TRN2", target_bir_lowering=False)

    def din(name, shape):
        return nc.dram_tensor(name, shape, F32, kind="ExternalInput").ap()

    x_d = din("x", [S, D])
    mem_d = din("mem", [256, D])
    par_d = din("params", [256, 128])
    cst_d = din("consts", [128, 64])
    relx_t = nc.dram_tensor("relx", [2, 16, 768], F32, kind="ExternalInput")
    w_in_d = din("w_in_ab", [2, D, 2560])
    poolw_d = din("pool_w", [2, 4, 128, 128])
    w_out_d = din("w_out_ab", [2, D, D])
    w_qkv_d = din("w_qkv", [2, D, 3 * D])
    w_oatt_d = din("w_o_att", [2, D, D])
    w_mq_d = din("w_mem_q", [4, D, D])
    w_mkv_d = din("w_mem_kv", [4, D, 2 * D])
    w_mo_d = din("w_mem_o", [4, D, D])
    w_ff1_d = din("w_ff1", [4, D, 4 * D])
    w_ff2_d = din("w_ff2", [4, 4 * D, D])
    y_d = nc.dram_tensor("y", [S, D], F32, kind="ExternalOutput").ap()
    xs_d = nc.dram_tensor("xs", [8, 128, S], F32).ap()

    def sb(name, shape, dt=F32):
        return nc.alloc_sbuf_tensor(name, shape, dt)

    ident = sb("ident", [128, 128]); b_ident = Buf()
    onesb = sb("onesb", [128, 128], BF16); b_onesb = Buf()
    onesf = sb("onesf", [128, 512]); b_onesf = Buf()
    pv = sb("pv", [128, 256]); b_pv = Buf()
    lbv = sb("lbv", [128, 32]); b_lbv = Buf()
    cst = sb("cst", [128, 64]); b_cst = Buf()
    mask = sb("mask", [128, 64]); b_mask = Buf()
    memT = sb("memT", [128, 8, 256], BF16); b_memT = Buf()
    kmT = sb("kmT", [128, 8, 256], BF16); b_kmT = bufs(8)
    vm = sb("vm", [128, 2, 1024], BF16); b_vm = bufs(2)
    xTs = [sb("xTa", [128, 8, 512]), sb("xTb", [128, 8, 512])]; b_xTs = [bufs(8), bufs(8)]
    X = {"t": xTs[0], "b": b_xTs[0]}
    hT = sb("hT", [128, 8, 512], BF16); b_hT = bufs(8)
    sq = sb("sq", [128, 8, 512], BF16); b_sq = bufs(8)
    yT = sb("yT", [128, 8, 512]); b_yT = bufs(8)
    rstd = sb("rstd", [128, 512]); b_rstd = Buf()
    big = sb("big", [128, 32, 512], BF16); b_big = bufs(32)
    pT2 = sb("pT2", [128, 2, 640], BF16); b_pT2 = bufs(2)
    wsl = [sb("wsl%d" % i, [128, 8, 512], BF16) for i in range(NW)]
    b_wsl = bufs(NW)
    MIXW = 14896
    mix = sb("mix", [128, MIXW])

    banks = [nc.alloc_psum_tensor("pb%d" % i, [128, 512], F32) for i in range(8)]
    b_bank = bufs(8)

    with ExitStack() as st:
        P = Prog(nc, st)
        c_w = [P.chan() for _ in range(NW)]
        c_x = P.chan(); c_xs = [P.chan(), P.chan()]; c_o = P.chan(); c_m1 = P.chan(); c_m2 = P.chan(); c_m3 = P.chan(); c_pw = P.chan(); c_e = [P.chan(), P.chan()]
        b_chain = Buf()

        wbi = [0]

        def wb():
            i = wbi[0] % 4
            wbi[0] += 1
            return banks[i], b_bank[i]

        def carve(off, n, dt=F32, **kw):
            v = mix[:, off:off + n]
            if dt == BF16:
                v = v.bitcast(BF16)
            return v

        def mm(out, lhsT, rhs, start, stop, reads, writes):
            P.op("pe", lambda e: e.matmul(out, lhsT=lhsT, rhs=rhs, start=start, stop=stop), reads, writes)

        def tp(out, in_, reads, writes):
            P.op("pe", lambda e: e.transpose(out, in_, ident[:]), [reads, b_ident], writes)

        def tpb(out, in_, reads, writes):
            P.op("pe", lambda e: e.transpose(out, in_, identb[:]), reads, writes)

        def act(out, in_, func, reads, writes, scale=1.0, bias=0.0):
            P.op("act", lambda e: e.activation(out=out, in_=in_, func=func, bias=bias, scale=scale), reads, writes)

        def tt(eng, out, in0, in1, op, reads, writes):
            P.op(eng, lambda e: e.tensor_tensor(out=out, in0=in0, in1=in1, op=op), reads, writes)

        def ts(eng, out, in0, s1, s2, op0, op1, reads, writes):
            if s2 is None:
                P.op(eng, lambda e: e.tensor_scalar(out=out, in0=in0, scalar1=s1, scalar2=None, op0=op0), reads, writes)
            else:
                P.op(eng, lambda e: e.tensor_scalar(out=out, in0=in0, scalar1=s1, scalar2=s2, op0=op0, op1=op1), reads, writes)

        def stt(out, in0, scalar, in1, op0, op1, reads, writes):
            P.op("dve", lambda e: e.scalar_tensor_tensor(out=out, in0=in0, scalar=scalar, in1=in1, op0=op0, op1=op1), reads, writes)

        def cp(eng, out, in_, reads, writes):
            if eng == "act":
                P.op("act", lambda e: e.copy(out=out, in_=in_), reads, writes)
            else:
                P.op(eng, lambda e: e.tensor_copy(out=out, in_=in_), reads, writes)

        def recip(out, in_, reads, writes):
            P.op("dve", lambda e: e.reciprocal(out=out, in_=in_), reads, writes)

        def memset(eng, ap, val, writes):
            P.op(eng, lambda e: e.memset(ap, val), None, writes)

        def wpiece(w2d, c0):
            return w2d.rearrange("(kc p) n -> p kc n", p=128)[:, :, c0:c0 + 512]

        seq = []
        for l in layers:
            ei = l // 2
            for j in range(4):
                seq.append(("kv%d" % j, wpiece(w_mkv_d[l], 512 * j)))
            for t in range(NT):
                if l % 2 == 0 and "mix" in SUBS:
                    for j in range(5):
                        seq.append(("win%d" % j, wpiece(w_in_d[ei], 512 * j)))
                    for j in range(2):
                        seq.append(("wout%d" % j, wpiece(w_out_d[ei], 512 * j)))
                elif "mix" in SUBS:
                    for j in range(6):
                        seq.append(("qkv%d" % j, wpiece(w_qkv_d[ei], 512 * j)))
                    for j in range(2):
                        seq.append(("wo%d" % j, wpiece(w_oatt_d[ei], 512 * j)))
                if "mem" in SUBS:
                    for j in range(2):
                        seq.append(("mq%d" % j, wpiece(w_mq_d[l], 512 * j)))
                    for j in range(2):
                        seq.append(("mo%d" % j, wpiece(w_mo_d[l], 512 * j)))
                if "mlp" in SUBS:
                    for j in range(8):
                        seq.append(("w1_%d" % j, wpiece(w_ff1_d[l], 512 * j)))
                    for h in range(2):
                        for r in range(4):
                            seq.append(("w2_%d_%d" % (h, r), wpiece(w_ff2_d[l][1024 * r:1024 * (r + 1), :], 512 * h)))
        rs = {"issued": 0, "used": 0}

        def ring_get(name):
            i = rs["used"]
            assert seq[i][0] == name, (seq[i][0], name)
            while rs["issued"] < min(i + NW, len(seq)):
                k = rs["issued"]
                P.dma("pool", c_w[k % NW], wsl[k % NW][:], seq[k][1], writes=[b_wsl[k % NW]])
                rs["issued"] += 1
            rs["used"] += 1
            return wsl[i % NW], b_wsl[i % NW]

        memset("pool", ident[:], 0.0, [b_ident])
        P.op("pool", lambda e: e.affine_select(out=ident[:], in_=ident[:], pattern=[[1, 128]], compare_op=ALU.not_equal,
                                               fill=1.0, base=0, channel_multiplier=-1), None, [b_ident])
        memset("pool", onesb[:], 1.0, [b_onesb])
        memset("pool", onesf[:], 1.0, [b_onesf])
        memset("pool", mask[:], 1.0, [b_mask])
        for hh in range(2):
            P.op("pool", lambda e, hh=hh: e.affine_select(out=mask[64 * hh:64 * hh + 64, :], in_=mask[64 * hh:64 * hh + 64, :],
                                                        pattern=[[1, 64]], compare_op=ALU.is_ge, fill=0.0, base=0,
                                                        channel_multiplier=-1), None, [b_mask])
        ptmp = carve(0, 256).rearrange("p (r c) -> p r c", c=128); b_ptmp = Buf()
        P.dma("sp", c_m1, ptmp, par_d.rearrange("(r p) c -> p r c", p=128), writes=[b_ptmp])
        P.dma("sp", c_m2, cst[:], cst_d, writes=[b_cst])
        bk, bb = wb()
        for r in range(2):
            tp(bk[:, r * 128:(r + 1) * 128], ptmp[:, r, :], [b_ptmp], [bb])
        cp("dve", pv[:], bk[:, 0:256], [bb], [b_pv])
        tmp = carve(256, 64); b_tmp = Buf()
        tt("dve", tmp[:, 0:4], pv[:, 212:216], pv[:, 208:212], ALU.subtract, [b_pv], [b_tmp])
        act(tmp[:, 4:8], tmp[:, 0:4], AF.Exp, [b_tmp], [b_tmp])
        act(tmp[:, 8:12], tmp[:, 0:4], AF.Exp, [b_tmp], [b_tmp], scale=-1.0)
        ts("dve", tmp[:, 4:12], tmp[:, 4:12], 1.0, None, ALU.add, None, [b_tmp], [b_tmp])
        recip(tmp[:, 4:12], tmp[:, 4:12], [b_tmp], [b_tmp])
        tt("dve", lbv[:, 0:4], tmp[:, 4:8], tmp[:, 4:8], ALU.subtract, [b_tmp], [b_lbv])
        tt("dve", tmp[:, 12:16], tmp[:, 4:8], tmp[:, 8:12], ALU.add, [b_tmp], [b_tmp])
        tt("dve", lbv[:, 4:8], tmp[:, 12:16], tmp[:, 4:8], ALU.subtract, [b_tmp], [b_lbv])
        ts("dve", lbv[:, 0:8], lbv[:, 0:8], 0.0, 1.0, ALU.max, ALU.min, [b_lbv], [b_lbv])
        ts("dve", lbv[:, 8:16], lbv[:, 0:8], -1.0, 1.0, ALU.mult, ALU.add, [b_lbv], [b_lbv])
        memf = carve(512, 2048).rearrange("p (m f) -> p m f", f=1024); b_memf = Buf()
        P.dma("sp", c_m3, memf, mem_d.rearrange("(m p) f -> p m f", p=128), writes=[b_memf])
        junk = big[:, 0:2, :].rearrange("p a b -> p (a b)")
        for mb in range(2):
            P.op("act", lambda e, mb=mb: e.activation(out=junk, in_=memf[:, mb, :], func=AF.Square,
                                                      accum_out=tmp[:, 16 + mb:17 + mb]), [b_memf], [b_big[0], b_big[1], b_tmp])
        act(tmp[:, 18:20], tmp[:, 16:18], AF.Ln, [b_tmp], [b_tmp], scale=1.0 / D, bias=EPS)
        act(tmp[:, 20:22], tmp[:, 18:20], AF.Exp, [b_tmp], [b_tmp], scale=-0.5)
        for mb in range(2):
            ts("dve", memf[:, mb, :], memf[:, mb, :], tmp[:, 20 + mb:21 + mb], None, ALU.mult, None, [b_memf, b_tmp], [b_memf])
        for c in range(8):
            bk, bb = wb()
            for mb in range(2):
                tp(bk[:, mb * 128:(mb + 1) * 128], memf[:, mb, c * 128:(c + 1) * 128], [b_memf], [bb])
            ts("dve", memT[:, c, :], bk[:, 0:256], pv[:, 192 + c:193 + c], None, ALU.mult, None, [bb, b_pv], [b_memT])

        def rstd_from(bk, bb, dim):
            act(rstd[:], bk[:], AF.Ln, [bb], [b_rstd], scale=1.0 / dim, bias=EPS)
            act(rstd[:], rstd[:], AF.Exp, [b_rstd], [b_rstd], scale=-0.5)

        def prenorm(l, j):
            for c in range(8):
                act(sq[:, c, :], X["t"][:, c, :], AF.Square, [X["b"][c]], [b_sq[c]])
            bk, bb = wb()
            for c in range(8):
                mm(bk[:], onesb[:], sq[:, c, :], c == 0, c == 7, [b_onesb, b_sq[c]], [bb])
            rstd_from(bk, bb, D)
            for c in range(8):
                g = pv[:, (l * 6 + j) * 8 + c:(l * 6 + j) * 8 + c + 1]
                stt(hT[:, c, :], X["t"][:, c, :], g, rstd[:], ALU.mult, ALU.mult, [X["b"][c], b_pv, b_rstd], [b_hT[c]])

        def post_evac(c, bk, bb):
            cp("act", yT[:, c, :], bk[:], [bb], [b_yT[c]])
            act(sq[:, c, :], bk[:], AF.Square, [bb], [b_sq[c]])

        def post_finish(l, j):
            bk, bb = wb()
            for c in range(8):
                mm(bk[:], onesb[:], sq[:, c, :], c == 0, c == 7, [b_onesb, b_sq[c]], [bb])
            rstd_from(bk, bb, D)
            for c in range(8):
                g = pv[:, (l * 6 + j) * 8 + c:(l * 6 + j) * 8 + c + 1]
                tt("dve", yT[:, c, :], yT[:, c, :], rstd[:], ALU.mult, [b_yT[c], b_rstd], [b_yT[c]])
                stt(X["t"][:, c, :], yT[:, c, :], g, X["t"][:, c, :], ALU.mult, ALU.add, [b_yT[c], b_pv, X["b"][c]], [X["b"][c]])

        def proj_fm(names, nout, src, b_src, evac):
            for oc in range(nout):
                if oc % 4 == 0:
                    pc, pb = ring_get(names[oc // 4])
                bk, bb = wb()
                for kc in range(8):
                    mm(bk[:], pc[:, kc, (oc % 4) * 128:(oc % 4 + 1) * 128], src[:, kc, :], kc == 0, kc == 7,
                       [pb, b_src[kc]], [bb])
                evac(oc, bk, bb)

        def proj_tm(name, src, b_src, evac):
            pc, pb = ring_get(name)
            for tb in range(4):
                bk, bb = wb()
                for kc in range(8):
                    mm(bk[:], src[:, kc, tb * 128:(tb + 1) * 128], pc[:, kc, :], kc == 0, kc == 7, [pb, b_src[kc]], [bb])
                evac(tb, bk, bb)

        ytok = yT[:].rearrange("p (tb h) f -> p tb (h f)", h=2)

        def load_x(l, t, first, par):
            if first:
                src = x_d[t * T:(t + 1) * T, :].rearrange("(tb p) f -> p tb f", p=128)
                P.dma("sp", c_x, ytok, src, writes=b_yT)
                for c in range(8):
                    bk, bb = wb()
                    for tb in range(4):
                        tp(bk[:, tb * 128:(tb + 1) * 128], ytok[:, tb, c * 128:(c + 1) * 128], [b_yT[2 * tb], b_yT[2 * tb + 1]], [bb])
                    cp("act" if c % 2 else "dve", xTs[par][:, c, :], bk[:], [bb], [b_xTs[par][c]])
            else:
                P.dma("sp", c_xs[par], xTs[par][:], xs_d[:, :, t * T:(t + 1) * T].rearrange("c p s -> p c s"), reads=[b_xs[t]], writes=b_xTs[par])

        def store_x(l, t, last):
            if last:
                for tb in range(4):
                    for h in range(2):
                        bk, bb = wb()
                        for cc in range(4):
                            c = 4 * h + cc
                            tp(bk[:, cc * 128:(cc + 1) * 128], X["t"][:, c, tb * 128:(tb + 1) * 128], [X["b"][c]], [bb])
                        cp("act" if h else "dve", ytok[:, tb, h * 512:(h + 1) * 512], bk[:], [bb], [b_yT[2 * tb + h]])
                dst = y_d[t * T:(t + 1) * T, :].rearrange("(tb p) f -> p tb f", p=128)
                return P.dma("sp", c_o, dst, ytok, reads=b_yT, writes=[b_chain])
            else:
                return P.dma("sp", c_o, xs_d[:, :, t * T:(t + 1) * T].rearrange("c p s -> p c s"), X["t"][:], reads=X["b"], writes=[b_xs[t], b_chain])

        b_xs = bufs(NT)

        def mem_attn(l):
            prenorm(l, 2)
            qT = big[:, 0:8, :]; b_q = b_big[0:8]
            oT = big[:, 8:16, :]; b_o = b_big[8:16]
            proj_fm(["mq0", "mq1"], 8, hT, b_hT,
                    lambda oc, bk, bb: cp("act" if oc % 2 else "dve", qT[:, oc, :], bk[:], [bb], [b_q[oc]]))
            for hd in range(4):
                pts = []
                for mb in range(2):
                    bk, bb = wb()
                    for i in range(2):
                        c = 2 * hd + i
                        mm(bk[:], kmT[:, c, mb * 128:(mb + 1) * 128], qT[:, c, :], i == 0, i == 1, [b_kmT[c], b_q[c]], [bb])
                    pt = big[:, 16 + mb, :]; bp = b_big[16 + mb]
                    act(pt, bk[:], AF.Exp, [bb], [bp], scale=1.0 / 16.0)
                    pts.append((pt, bp))
                dk, db = wb()
                for mb in range(2):
                    mm(dk[:], onesb[:], pts[mb][0], mb == 0, mb == 1, [b_onesb, pts[mb][1]], [db])
                rd = big[:, 18:20, :].rearrange("p a b -> p (a b)").bitcast(F32); brd = [b_big[18], b_big[19]]
                recip(rd, dk[:], [db], brd)
                for dc in range(2):
                    c = 2 * hd + dc
                    bk, bb = wb()
                    for mb in range(2):
                        mm(bk[:], vm[:, mb, c * 128:(c + 1) * 128], pts[mb][0], mb == 0, mb == 1, [b_vm[mb], pts[mb][1]], [bb])
                    tt("dve", oT[:, c, :], bk[:], rd, ALU.mult, [bb, brd], [b_o[c]])
            proj_fm(["mo0", "mo1"], 8, oT, b_o, post_evac)
            post_finish(l, 3)

        def mlp(l):
            prenorm(l, 4)

            def ev1(hc, bk, bb):
                act(big[:, hc, :], bk[:], AF.Relu, [bb], [b_big[hc]])
                tt("dve", big[:, hc, :], big[:, hc, :], big[:, hc, :], ALU.mult, [b_big[hc]], [b_big[hc]])
            proj_fm(["w1_%d" % j for j in range(8)], 32, hT, b_hT, ev1)
            for h in range(2):
                ab = (4, 5, 6, 7) if h == 0 else (0, 1, 2, 3)
                for r in range(4):
                    pc, pb = ring_get("w2_%d_%d" % (h, r))
                    for i in range(4):
                        for kk in range(8):
                            hc = 8 * r + kk
                            mm(banks[ab[i]][:], pc[:, kk, i * 128:(i + 1) * 128], big[:, hc, :],
                               r == 0 and kk == 0, r == 3 and kk == 7, [pb, b_big[hc]], [b_bank[ab[i]]])
                for i in range(4):
                    post_evac(4 * h + i, banks[ab[i]], b_bank[ab[i]])
            post_finish(l, 5)

        o = [0]

        def take(n):
            a = o[0]
            o[0] += n
            return a
        E_UB = take(4 * 528); E_QS = take(1024); E_KS = take(1024); E_SG = take(1024); E_VT = take(1024)
        E_TMP = [take(512) for _ in range(7)]
        E_KT = [take(128) for _ in range(2)]
        E_G = take(1024); E_SB = take(512); E_S = take(512); E_AM = [take(256) for _ in range(2)]
        E_PT = [take(528) for _ in range(3)]; E_FV = take(64); E_PW = take(256); E_FVH = take(96)
        assert o[0] <= MIXW, o[0]
        ub = carve(E_UB, 4 * 528).rearrange("p (g t) -> p g t", t=528); b_ub = bufs(4)
        qs = carve(E_QS, 1024, BF16).rearrange("p (h t) -> p h t", t=512); b_qs = bufs(4)
        ks = carve(E_KS, 1024, BF16).rearrange("p (h t) -> p h t", t=512); b_ks = bufs(4)
        sg = carve(E_SG, 1024, BF16).rearrange("p (h t) -> p h t", t=512); b_sg = bufs(4)
        vt = carve(E_VT, 1024, BF16).rearrange("p (b f) -> p b f", f=512); b_vt = bufs(4)
        tmpf = [carve(a, 512) for a in E_TMP]; b_tmpf = bufs(7)
        kst = [carve(a, 128, BF16).rearrange("p (b d) -> p b d", d=128) for a in E_KT]; b_kst = bufs(2)
        Gs = carve(E_G, 1024).rearrange("p (j e) -> p j e", e=128); b_Gs = Buf()
        Sb = carve(E_SB, 512, BF16).rearrange("p (j e) -> p j e", e=128); b_Sb = Buf()
        Sst = carve(E_S, 512).rearrange("p (h e) -> p h e", e=128); b_S = bufs(4)
        Am = [carve(a, 256, BF16) for a in E_AM]; b_Am = bufs(2)
        ptm = [carve(a, 528) for a in E_PT]; b_ptm = bufs(3)
        fv = carve(E_FV, 64); b_fv = Buf()
        poolw = carve(E_PW, 256, BF16).rearrange("p (g d) -> p g d", d=128); b_poolw = Buf()
        fvh = carve(E_FVH, 96).rearrange("p (h a) -> p h a", a=24); b_fvh = bufs(4)

        def even_layer_init(l):
            ei = l // 2
            if MIXSTOP < 0.1:
                return
            pwst = tmpf[0].rearrange("p (g d) -> p g d", d=128)
            P.dma("sp", c_pw, pwst, poolw_d[ei].rearrange("g c d -> c g d"), writes=[b_tmpf[0]])
            cp("dve", poolw, pwst, [b_tmpf[0]], [b_poolw])
            for hd in range(4):
                memset("dve", Sst[:, hd, :], 0.0, [b_S[hd]])
            for g in range(4):
                memset("dve", ub[:, g, 0:16], 0.0, [b_ub[g]])
            for i in range(2):
                memset("dve", Am[i], 0.0, [b_Am[i]])

        def even_mixer(l, t):
            ei = l // 2
            prenorm(l, 0)
            def gate(th, name, f):
                if MIXSTOP >= th:
                    f()
                else:
                    ring_get(name)
            gate(0.2, "win0", lambda: proj_fm(["win0"], 4, hT, b_hT,
                    lambda oc, bk, bb: cp("act", ub[:, oc, 16:528], bk[:], [bb], [b_ub[oc]])))
            gate(0.3, "win1", lambda: proj_fm(["win1"], 4, hT, b_hT,
                    lambda oc, bk, bb: cp("dve", qs[:, oc, :], bk[:], [bb], [b_qs[oc]])))

            def ev_fz(hd, bk, bb):
                E_, f_, k_, lf_, B_, bq_, ex_ = tmpf
                bE, bf_, bk_, blf, bB, bbq, bex = b_tmpf
                lb = lbv[:, ei * 4 + hd:ei * 4 + hd + 1]
                oml = lbv[:, 8 + ei * 4 + hd:8 + ei * 4 + hd + 1]
                act(E_, bk[:], AF.Sigmoid, [bb], [bE])
                P.op("act", lambda e: e.activation(out=f_, in_=E_, func=AF.Identity, scale=oml, bias=lb), [bE, b_lbv], [bf_])
                act(k_, f_, AF.Identity, [bf_], [bk_], scale=-1.0, bias=1.0)
                act(lf_, f_, AF.Ln, [bf_], [blf])
                P.op("dve", lambda e: e.tensor_tensor_scan(out=B_, data0=onesf[:], data1=lf_, initial=0.0,
                                                           op0=ALU.mult, op1=ALU.add), [b_onesf, blf], [bB])
                B3 = B_.rearrange("p (j s) -> p j s", s=64)
                cmid = B3[:, :, 31:32]
                bend = B3[:, :, 63:64]
                fv3 = fv[:, 0:64].rearrange("p (a j) -> p a j", j=8)
                memset("dve", fv[:, 24:25], 0.0, [b_fv])
                cp("dve", fv[:, 25:32].rearrange("p (j o) -> p j o", o=1), B3[:, 0:7, 63:64], [bB], [b_fv])
                bs3 = fv[:, 24:32].rearrange("p (j o) -> p j o", o=1)
                tt("dve", fv[:, 0:8].rearrange("p (j o) -> p j o", o=1), cmid, bs3, ALU.subtract, [bB, b_fv], [b_fv])
                tt("dve", fv[:, 8:16].rearrange("p (j o) -> p j o", o=1), bend, bs3, ALU.subtract, [bB, b_fv], [b_fv])
                tt("dve", fv[:, 16:24].rearrange("p (j o) -> p j o", o=1), bend, cmid, ALU.subtract, [bB], [b_fv])
                act(fv[:, 32:56], fv[:, 0:24], AF.Exp, [b_fv], [b_fv])
                tt("dve", bq_.rearrange("p (j s) -> p j s", s=64), B3, cmid.to_broadcast([128, 8, 64]), ALU.subtract, [bB], [bbq])
                act(ex_, bq_, AF.Exp, [bbq], [bex])
                tt("dve", qs[:, hd, :], qs[:, hd, :], ex_, ALU.mult, [b_qs[hd], bex], [b_qs[hd]])
                act(ex_, bq_, AF.Exp, [bbq], [bex], scale=-1.0)
                tt("dve", ks[:, hd, :], k_, ex_, ALU.mult, [bk_, bex], [b_ks[hd]])
                cp("dve", fvh[:, hd, :], fv[:, 32:56], [b_fv], [b_fvh[hd]])
            gate(0.4, "win2", lambda: proj_fm(["win2"], 4, hT, b_hT, ev_fz))
            gate(0.5, "win3", lambda: proj_tm("win3", hT, b_hT, lambda tb, bk, bb: cp("act", vt[:, tb, :], bk[:], [bb], [b_vt[tb]])))

            def ev_g(hd, bk, bb):
                act(sg[:, hd, :], bk[:], AF.Silu, [bb], [b_sg[hd]])
            gate(1, "win4", lambda: proj_fm(["win4"], 4, hT, b_hT, ev_g))

            mixin = big[:, 0:8, :]; b_mi = b_big[0:8]
            for g in range(4 if MIXSTOP >= 2 else 0):
                w = 2 << g
                cur = ub[:, g, :]; bcur = b_ub[g]
                lo = 0
                sh = 1
                k = 0
                while sh < w:
                    nxt = ptm[k % 2]; bn = b_ptm[k % 2]
                    lo2 = lo + sh
                    tt("dve", nxt[:, lo2:528], cur[:, lo2:528], cur[:, lo2 - sh:528 - sh], ALU.add, [bcur], [bn])
                    cur, bcur, lo, sh, k = nxt, bn, lo2, sh * 2, k + 1
                pl = hT[:, g, :]
                stt(pl, cur[:, 16:528], 1.0 / w, ub[:, g, 16:528], ALU.mult, ALU.subtract, [bcur, b_ub[g]], [b_hT[g]])
                if t == 0:
                    fx = ptm[2][:, 0:16]
                    tt("dve", fx, cur[:, 16:32], cst[:, g * 16:(g + 1) * 16], ALU.mult, [bcur, b_cst], [b_ptm[2]])
                    tt("dve", pl[:, 0:16], fx, ub[:, g, 16:32], ALU.subtract, [b_ptm[2], b_ub[g]], [b_hT[g]])
                cp("dve", ub[:, g, 0:16], ub[:, g, 512:528], [b_ub[g]], [b_ub[g]])
                bk, bb = wb()
                mm(bk[:], poolw[:, g, :], pl, True, True, [b_poolw, b_hT[g]], [bb])
                P.op("act", lambda e, g=g, bk=bk: e.activation(out=mixin[:, g, :], in_=bk[:], func=AF.Copy, scale=pv[:, 200 + ei * 4 + g:201 + ei * 4 + g]),
                     [bb, b_pv], [b_mi[g]])
            for hd in range(4 if MIXSTOP >= 3 else 0):
                F1 = fvh[:, hd, 0:8]; F2 = fvh[:, hd, 8:16]; F3 = fvh[:, hd, 16:24]
                ak, ab = wb()
                for j in range(8):
                    r0 = 64 * (j % 2)
                    mm(ak[r0:r0 + 64, j * 64:(j + 1) * 64], ks[:, hd, j * 64:(j + 1) * 64], qs[:, hd, j * 64:(j + 1) * 64], True, True,
                       [b_ks[hd], b_qs[hd]], [ab])
                am = Am[hd % 2]; bam = b_Am[hd % 2]
                for r in range(2):
                    r0 = 64 * r
                    src = ak[r0:r0 + 64, :].rearrange("p (j two s) -> p j two s", two=2, s=64)[:, :, r, :]
                    dst = am[r0:r0 + 64, :].rearrange("p (j two s) -> p j two s", two=2, s=64)[:, :, r, :]
                    tt("dve", dst, src, mask[r0:r0 + 64, :].rearrange("p (o s) -> p o s", o=1).to_broadcast([64, 4, 64]), ALU.mult,
                       [ab, b_mask], [bam])
                for half in range(2):
                    bk, bb = wb()
                    bkb = bk[:].bitcast(BF16)
                    for q in range(2):
                        blk = 2 * half + q
                        tpb(bkb[:, q * 128:(q + 1) * 128], ks[:, hd, blk * 128:(blk + 1) * 128], [b_ks[hd], b_identb], [bb])
                    cp("act", kst[half].rearrange("p b d -> p (b d)"), bkb[:, 0:256], [bb], [b_kst[half]])
                Gs4 = Gs.rearrange("p (jj two) e -> p jj two e", two=2)
                for r in range(2):
                    r0 = 64 * r
                    bk, bb = wb()
                    for jj in range(4):
                        blk = jj
                        mm(bk[:, jj * 128:(jj + 1) * 128], kst[blk // 2][r0:r0 + 64, blk % 2, :], vt[r0:r0 + 64, blk, hd * 128:(hd + 1) * 128],
                           True, True, [b_kst[blk // 2], b_vt[blk]], [bb])
                    f3 = F3.rearrange("p (jj two) -> p jj two", two=2)[:, :, r:r + 1]
                    tt("dve", Gs4[:, :, r, :], bk[:].rearrange("p (j e) -> p j e", e=128),
                       f3.to_broadcast([128, 4, 128]), ALU.mult, [bb, b_fvh[hd]], [b_Gs])
                for j in range(8):
                    P.op("act", lambda e, j=j, hd=hd, F1=F1: e.activation(out=Sb[:, j, :], in_=Sst[:, hd, :], func=AF.Copy, scale=F1[:, j:j + 1]),
                         [b_S[hd], b_fvh[hd]], [b_Sb])
                    stt(Sst[:, hd, :], Sst[:, hd, :], F2[:, j:j + 1], Gs[:, j, :], ALU.mult, ALU.add, [b_S[hd], b_fvh[hd], b_Gs], [b_S[hd]])
                ok, ob = wb()
                for blk in range(4):
                    mm(ok[:, blk * 128:(blk + 1) * 128], vt[:, blk, hd * 128:(hd + 1) * 128], am[:, blk * 128:(blk + 1) * 128], True, False,
                       [b_vt[blk], bam], [ob])
                    for r in range(2):
                        j = 2 * blk + r
                        mm(ok[:, j * 64:(j + 1) * 64], Sb[:, j, :], qs[:, hd, j * 64:(j + 1) * 64], False, r == 1, [b_Sb, b_qs[hd]], [ob])
                oc_ = tmpf[2]; boc = b_tmpf[2]
                osq = hT[:, 4 + hd, :]
                cp("act", oc_, ok[:], [ob], [boc])
                act(osq, oc_, AF.Square, [boc], [b_hT[4 + hd]])
                sk, sbb = wb()
                mm(sk[:], onesb[:], osq, True, True, [b_onesb, b_hT[4 + hd]], [sbb])
                rstd_from(sk, sbb, 128)
                stt(oc_, oc_, pv[:, 216 + ei:217 + ei], rstd[:], ALU.mult, ALU.mult, [boc, b_pv, b_rstd], [boc])
                tt("dve", mixin[:, 4 + hd, :], oc_, sg[:, hd, :], ALU.mult, [boc, b_sg[hd]], [b_mi[4 + hd]])
            if MIXSTOP < 6:
                for c in range(8):
                    memset("dve", mixin[:, c, :], 0.0, [b_mi[c]])
            if MIXPART == "pool":
                for c in range(4, 8):
                    memset("dve", mixin[:, c, :], 0.0, [b_mi[c]])
            if MIXPART == "rec":
                for c in range(4):
                    memset("dve", mixin[:, c, :], 0.0, [b_mi[c]])
            proj_fm(["wout0", "wout1"], 8, mixin, b_mi, post_evac)
            post_finish(l, 1)

        antiJ = sb("antiJ", [128, 128]); b_antiJ = Buf()
        memset("pool", antiJ[:], 0.0, [b_antiJ])
        P.op("pool", lambda e: e.affine_select(out=antiJ[:], in_=antiJ[:], pattern=[[1, 128]], compare_op=ALU.not_equal,
                                               fill=1.0, base=-127, channel_multiplier=1), None, [b_antiJ])
        identb = sb("identb", [128, 128], BF16); b_identb = Buf()
        cp("dve", identb[:], ident[:], [b_ident], [b_identb])

        O_KT = 0; O_VT = 4096; O_E = 8192; O_BR = 8192 + 5120
        kT = carve(O_KT, 4096, BF16).rearrange("p (s c t) -> p s c t", s=2, c=8); b_kT = [bufs(8), bufs(8)]
        vtk = carve(O_VT, 4096, BF16).rearrange("p (s b f) -> p s b f", s=2, b=4); b_vtk = [bufs(4), bufs(4)]
        Et = carve(O_E, 5120, BF16).rearrange("p (h m) -> p h m", m=640); b_E = bufs(16)
        braw = [carve(O_BR + 640 * i, 640) for i in range(2)]; b_braw = bufs(2)
        assert O_BR + 1280 <= MIXW

        def odd_layer_init(l):
            oi = l // 2
            for hh in range(16):
                br = braw[hh % 2]; bbr = b_braw[hh % 2]
                src = bass.AP(relx_t, (oi * 16 + hh) * 768, [[1, 128], [1, 640]])
                P.dma("sp", c_e[hh % 2], br, src, writes=[bbr])
                bA, bAb = wb()
                mm(bA[:], antiJ[:], br[:, 0:512], True, True, [b_antiJ, bbr], [bAb])
                bB, bBb = wb()
                mm(bB[:, 0:128], antiJ[:], br[:, 512:640], True, True, [b_antiJ, bbr], [bBb])
                act(Et[:, hh, 0:512], bA[:], AF.Exp, [bAb], [b_E[hh]])
                act(Et[:, hh, 512:640], bB[:, 0:128], AF.Exp, [bBb], [b_E[hh]])
                memset("dve", Et[64:128, hh, 0:64], 0.0, [b_E[hh]])
                memset("dve", Et[0:64, hh, 576:640], 0.0, [b_E[hh]])

        def odd_mixer(l, t):
            slot = t % 2
            prenorm(l, 0)
            qT = big[:, 0:8, :]; b_q = b_big[0:8]
            oT = big[:, 8:16, :]; b_o = b_big[8:16]
            proj_fm(["qkv0", "qkv1"], 8, hT, b_hT,
                    lambda oc, bk, bb: cp("act" if oc % 2 else "dve", qT[:, oc, :], bk[:], [bb], [b_q[oc]]))
            proj_fm(["qkv2", "qkv3"], 8, hT, b_hT,
                    lambda oc, bk, bb: cp("act" if oc % 2 else "dve", kT[:, slot, oc, :], bk[:], [bb], [b_kT[slot][oc]]))
            for h in range(2):
                proj_tm("qkv%d" % (4 + h), hT, b_hT,
                        lambda tb, bk, bb, h=h: cp("act" if tb % 2 else "dve", vtk[:, slot, tb, h * 512:(h + 1) * 512], bk[:], [bb], [b_vtk[slot][tb]]))
            items = [(qb, hp, r) for qb in range(4) for hp in range(8) for r in range(2)]

            def emit_scores(it):
                qb, hp, r = it
                G = 4 * t + qb
                nk = min(G, 4) + 1
                r0 = 64 * r
                sA, sAb = wb()
                sB = sBb = None
                if nk == 5:
                    sB, sBb = wb()
                kinfo = []
                for k in range(nk):
                    g = G - k
                    gs = (g // 4) % 2; gl = g % 4
                    kinfo.append((gs, gl))
                    bk_, bb_ = (sA, sAb) if k < 4 else (sB, sBb)
                    col = (k % 4) * 128
                    mm(bk_[:, col:col + 128], kT[r0:r0 + 64, gs, hp, gl * 128:(gl + 1) * 128], qT[r0:r0 + 64, hp, qb * 128:(qb + 1) * 128],
                       True, True, [b_kT[gs][hp], b_q[hp]], [bb_])
                return (it, nk, sA, sAb, sB, sBb, kinfo)

            def emit_rest(ctx, hi):
                (qb, hp, r), nk, sA, sAb, sB, sBb, kinfo = ctx
                hh = 2 * hp + r
                r0 = 64 * r
                pt = pT2[:, hi % 2, :]; bp = b_pT2[hi % 2]
                na = min(nk, 4) * 128
                act(pt[:, 0:na], sA[:, 0:na], AF.Exp, [sAb], [bp], scale=0.125)
                if nk == 5:
                    act(pt[:, 512:640], sB[:, 0:128], AF.Exp, [sBb], [bp], scale=0.125)
                tt("dve", pt[:, 0:nk * 128], pt[:, 0:nk * 128], Et[:, hh, 0:nk * 128], ALU.mult, [bp, b_E[hh]], [bp])
                po = banks[4 + hp // 4]; pob = b_bank[4 + hp // 4]
                pd = banks[6 + hp // 4]; pdb = b_bank[6 + hp // 4]
                cs = (hp % 4) * 128
                for k in range(nk):
                    gs, gl = kinfo[k]
                    mm(po[r0:r0 + 64, cs:cs + 128], vtk[:, gs, gl, hh * 64:(hh + 1) * 64], pt[:, k * 128:(k + 1) * 128], k == 0, k == nk - 1,
                       [b_vtk[gs][gl], bp], [pob])
                for k in range(nk):
                    mm(pd[r0:r0 + 64, cs:cs + 128], onesb[:, 0:64], pt[:, k * 128:(k + 1) * 128], k == 0, k == nk - 1,
                       [b_onesb, bp], [pdb])
                if hp == 7 and r == 1:
                    for h in range(2):
                        rd = rstd[:]
                        recip(rd, banks[6 + h][:], [b_bank[6 + h]], [b_rstd])
                        tt("dve", oT[:, 4 * h:4 * h + 4, qb * 128:(qb + 1) * 128], banks[4 + h][:].rearrange("p (c q) -> p c q", q=128),
                           rd.rearrange("p (c q) -> p c q", q=128), ALU.mult, [b_bank[4 + h], b_rstd], b_o[4 * h:4 * h + 4])

            ctx = emit_scores(items[0])
            for i in range(len(items)):
                nxt = emit_scores(items[i + 1]) if i + 1 < len(items) else None
                emit_rest(ctx, i)
                ctx = nxt
            proj_fm(["wo0", "wo1"], 8, oT, b_o, post_evac)
            post_finish(l, 1)

        out_toks = []
        loaded = set()
        for li, l in enumerate(layers):
            first = li == 0
            last = li == len(layers) - 1
            P.barrier()
            for oc in range(8):
                if oc % 4 == 0:
                    pc, pb = ring_get("kv%d" % (oc // 4))
                bk, bb = wb()
                for kc in range(8):
                    mm(bk[:, 0:256], pc[:, kc, (oc % 4) * 128:(oc % 4 + 1) * 128], memT[:, kc, :], kc == 0, kc == 7, [pb, b_memT], [bb])
                cp("act", kmT[:, oc, :], bk[:, 0:256], [bb], [b_kmT[oc]])
            for h in range(2):
                pc, pb = ring_get("kv%d" % (2 + h))
                for mb in range(2):
                    bk, bb = wb()
                    for kc in range(8):
                        mm(bk[:], memT[:, kc, mb * 128:(mb + 1) * 128], pc[:, kc, :], kc == 0, kc == 7, [pb, b_memT], [bb])
                    cp("dve", vm[:, mb, h * 512:(h + 1) * 512], bk[:], [bb], [b_vm[mb]])
            if l % 2 == 0:
                even_layer_init(l)
            else:
                odd_layer_init(l)
            for t in range(NT):
                g = li * NT + t
                par = g % 2
                if g not in loaded:
                    load_x(l, t, first, par)
                    loaded.add(g)
                if g + 1 < len(layers) * NT and (g + 1) // NT > 0 and (g + 1) not in loaded:
                    l2 = layers[(g + 1) // NT]; t2 = (g + 1) % NT
                    if (g + 1) // NT == li or t2 == 0 and False:
                        load_x(l2, t2, False, 1 - par)
                        loaded.add(g + 1)
                X["t"] = xTs[par]; X["b"] = b_xTs[par]
                if "mix" in SUBS:
                    if l % 2 == 0:
                        even_mixer(l, t)
                    else:
                        odd_mixer(l, t)
                if "mem" in SUBS:
                    mem_attn(l)
                if "mlp" in SUBS:
                    mlp(l)
                tok = store_x(l, t, last)
                if last:
                    out_toks.append(tok)
        P.ops["sp"].append(Op(None, [out_toks[-1]]))
        stats = P.emit()
    return nc, stats


def _host_prep(inputs):
    f = lambda a: np.ascontiguousarray(np.asarray(a, dtype=np.float32))
    ng = f(inputs["norm_gains"]).reshape(192, 128)
    params = np.concatenate([
        ng,
        f(inputs["mem_norm_gain"]).reshape(8, 128),
        f(inputs["pool_scale"]).reshape(8, 128),
        f(inputs["rec_lb_logits"]).reshape(8, 128),
        f(inputs["rec_out_gain"]).reshape(2, 128),
        np.zeros((38, 128), np.float32)], axis=0)
    consts = np.zeros((128, 64), np.float32)
    for g, w in enumerate((2, 4, 8, 16)):
        consts[:, g * 16:(g + 1) * 16] = 1.0 / np.minimum(np.arange(1, 17), w)
    idx = np.clip(np.arange(768) - 127, -256, 256) + 256
    relx = f(inputs["rel_bias"])[:, :, idx]
    shared = {"params": params, "consts": consts, "relx": np.ascontiguousarray(relx)}
    for k in ("w_in_ab", "pool_w", "w_out_ab", "w_qkv", "w_o_att", "w_mem_q", "w_mem_kv", "w_mem_o", "w_ff1", "w_ff2"):
        shared[k] = f(inputs[k])
    return shared


_CACHE = {}


def kernel(**inputs):
    x = np.asarray(inputs["x"], dtype=np.float32)
    mem = np.asarray(inputs["mem"], dtype=np.float32)
    B, S, _ = x.shape
    shared = _host_prep(inputs)
    key = (S,)
    if key not in _CACHE:
        _CACHE[key] = build(S, [0, 1, 2, 3])[0]
    nc = _CACHE[key]
    in_maps = []
    for b in range(B):
        m = dict(shared)
        m["x"] = np.ascontiguousarray(x[b])
        m["mem"] = np.ascontiguousarray(mem[b])
        in_maps.append(m)
    res = run_bass_kernel_spmd(nc, in_maps, core_ids=list(range(B)))
    return np.stack([res.results[b]["y"] for b in range(B)], axis=0).astype(np.float32)
```

```python
import sys
from contextlib import ExitStack
import numpy as np
import concourse.bass as bass
import concourse.mybir as mybir
from concourse.bass_utils import run_bass_kernel_spmd

F32 = mybir.dt.float32
BF16 = mybir.dt.bfloat16
AF = mybir.ActivationFunctionType
ALU = mybir.AluOpType

D = 1024
T = 512
NW = 4
EPS = 1e-6
ENGS = ("pe", "act", "dve", "pool", "sp")
SUBS = ("mix", "mem", "mlp")
MIXPART = "all"
MIXSTOP = 6


class Buf:
    __slots__ = ("w", "r")

    def __init__(self):
        self.w = None
        self.r = {}


def bufs(n):
    return [Buf() for _ in range(n)]


class Chan:
    __slots__ = ("sem", "count")

    def __init__(self, sem):
        self.sem = sem
        self.count = 0


class Op:
    __slots__ = ("fn", "deps", "signal", "cnt", "dma", "src")

    def __init__(self, fn, deps, dma=None):
        try:
            f = sys._getframe(2)
            self.src = (f.f_lineno, f.f_back.f_lineno if f.f_back else 0, f.f_back.f_back.f_lineno if f.f_back and f.f_back.f_back else 0)
        except ValueError:
            self.src = None
        self.fn = fn
        self.deps = deps
        self.signal = False
        self.cnt = 0
        self.dma = dma


def _flat(x, out):
    if x is None:
        return out
    if isinstance(x, Buf):
        out.append(x)
        return out
    for y in x:
        _flat(y, out)
    return out


class Prog:
    def __init__(self, nc, stack):
        self.nc = nc
        self.ops = {e: [] for e in ENGS}
        self.sem = {e: stack.enter_context(nc.semaphore("s_" + e)) for e in ENGS}
        self.stack = stack
        self.chans = []

    def chan(self):
        c = Chan(self.stack.enter_context(self.nc.semaphore("c%d" % len(self.chans))))
        self.chans.append(c)
        return c

    def _collect(self, eng, rd, wr):
        deps = {}
        for b in rd:
            if b.w is not None:
                deps[b.w[1:]] = b.w
        for b in wr:
            if b.w is not None:
                deps[b.w[1:]] = b.w
            for t in b.r.values():
                deps[t[1:]] = t
        out = []
        for t in deps.values():
            if t[0] == "e":
                if t[1] == eng and eng == "pe":
                    continue
                self.ops[t[1]][t[2]].signal = True
            out.append(t)
        return out

    def _mark(self, tok, rd, wr):
        key = tok[1]
        for b in rd:
            b.r[key] = tok
        for b in wr:
            b.w = tok
            b.r = {}

    def op(self, eng, fn, reads=None, writes=None):
        rd = _flat(reads, [])
        wr = _flat(writes, [])
        deps = self._collect(eng, rd, wr)
        seq = len(self.ops[eng])
        self.ops[eng].append(Op(fn, deps))
        self._mark(("e", eng, seq), rd, wr)

    def dma(self, q, chan, out_ap, in_ap, reads=None, writes=None):
        rd = _flat(reads, [])
        wr = _flat(writes, [])
        deps = self._collect(q, rd, wr)
        chan.count += 16
        tok = ("d", chan, chan.count)
        self.ops[q].append(Op(lambda e: e.dma_start(out=out_ap, in_=in_ap), deps, dma=(chan, chan.count)))
        self._mark(tok, rd, wr)
        return tok

    def barrier(self):
        last = {}
        for e in ENGS:
            if self.ops[e]:
                for i in range(len(self.ops[e]) - 1, -1, -1):
                    if self.ops[e][i].fn is not None and self.ops[e][i].dma is None:
                        last[e] = i
                        break
        for e in ENGS:
            deps = []
            for f, i in last.items():
                if f != e:
                    self.ops[f][i].signal = True
                    deps.append(("e", f, i))
            for c in self.chans:
                if c.count:
                    deps.append(("d", c, c.count))
            self.ops[e].append(Op(None, deps))

    def emit(self):
        nc = self.nc
        LIM = 16000
        nsig = {}
        for e in ENGS:
            c = 0
            for o in self.ops[e]:
                if o.signal:
                    c += 1
                o.cnt = c
            nsig[e] = c
        sems = {}
        for e in ENGS:
            sems[e] = [self.sem[e]]
            for i in range(1, (nsig[e] + LIM - 1) // LIM):
                sems[e].append(self.stack.enter_context(nc.semaphore("s_%s_%d" % (e, i))))
        stats = {}
        with nc.Block() as block:
            def run(eng, e):
                seen = {}
                nw = 0
                for o in self.ops[eng]:
                    for t in o.deps:
                        if t[0] == "e":
                            key = t[1]
                            val = self.ops[t[1]][t[2]].cnt
                            if seen.get(key, 0) >= val:
                                continue
                            seen[key] = val
                            e.wait_ge(sems[t[1]][(val - 1) // LIM], (val - 1) % LIM + 1)
                        else:
                            key = id(t[1])
                            val = t[2]
                            if seen.get(key, 0) >= val:
                                continue
                            seen[key] = val
                            e.wait_ge(t[1].sem, val)
                        nw += 1
                    if o.fn is None:
                        continue
                    try:
                        ins = o.fn(e)
                    except Exception:
                        print("FAILED OP", eng, "src lines", o.src, flush=True)
                        raise
                    if o.dma is not None:
                        ins.then_inc(o.dma[0].sem, 16)
                    elif o.signal:
                        ins.then_inc(sems[eng][(o.cnt - 1) // LIM], 1)
                stats[eng] = (len(self.ops[eng]), nw, nsig[eng])

            @block.tensor
            def _(e):
                run("pe", e)

            @block.scalar
            def _(e):
                run("act", e)

            @block.vector
            def _(e):
                run("dve", e)

            @block.gpsimd
            def _(e):
                run("pool", e)

            @block.sync
            def _(e):
                run("sp", e)
        return stats


def build(S, layers):
    NT = S // T
    nc = bass.Bass("TRN2", target_bir_lowering=False)

    def din(name, shape):
        return nc.dram_tensor(name, shape, F32, kind="ExternalInput").ap()

    x_d = din("x", [S, D])
    mem_d = din("mem", [256, D])
    par_d = din("params", [256, 128])
    cst_d = din("consts", [128, 64])
    relx_t = nc.dram_tensor("relx", [2, 16, 768], F32, kind="ExternalInput")
    w_in_d = din("w_in_ab", [2, D, 2560])
    poolw_d = din("pool_w", [2, 4, 128, 128])
    w_out_d = din("w_out_ab", [2, D, D])
    w_qkv_d = din("w_qkv", [2, D, 3 * D])
    w_oatt_d = din("w_o_att", [2, D, D])
    w_mq_d = din("w_mem_q", [4, D, D])
    w_mkv_d = din("w_mem_kv", [4, D, 2 * D])
    w_mo_d = din("w_mem_o", [4, D, D])
    w_ff1_d = din("w_ff1", [4, D, 4 * D])
    w_ff2_d = din("w_ff2", [4, 4 * D, D])
    y_d = nc.dram_tensor("y", [S, D], F32, kind="ExternalOutput").ap()
    xs_d = nc.dram_tensor("xs", [8, 128, S], F32).ap()

    def sb(name, shape, dt=F32):
        return nc.alloc_sbuf_tensor(name, shape, dt)

    ident = sb("ident", [128, 128]); b_ident = Buf()
    onesb = sb("onesb", [128, 128], BF16); b_onesb = Buf()
    onesf = sb("onesf", [128, 512]); b_onesf = Buf()
    pv = sb("pv", [128, 256]); b_pv = Buf()
    lbv = sb("lbv", [128, 32]); b_lbv = Buf()
    cst = sb("cst", [128, 64]); b_cst = Buf()
    mask = sb("mask", [128, 64]); b_mask = Buf()
    memT = sb("memT", [128, 8, 256], BF16); b_memT = Buf()
    kmT = sb("kmT", [128, 8, 256], BF16); b_kmT = bufs(8)
    vm = sb("vm", [128, 2, 1024], BF16); b_vm = bufs(2)
    xTs = [sb("xTa", [128, 8, 512]), sb("xTb", [128, 8, 512])]; b_xTs = [bufs(8), bufs(8)]
    X = {"t": xTs[0], "b": b_xTs[0]}
    hT = sb("hT", [128, 8, 512], BF16); b_hT = bufs(8)
    sq = sb("sq", [128, 8, 512], BF16); b_sq = bufs(8)
    yT = sb("yT", [128, 8, 512]); b_yT = bufs(8)
    rstd = sb("rstd", [128, 512]); b_rstd = Buf()
    big = sb("big", [128, 32, 512], BF16); b_big = bufs(32)
    pT2 = sb("pT2", [128, 2, 640], BF16); b_pT2 = bufs(2)
    wsl = [sb("wsl%d" % i, [128, 8, 512], BF16) for i in range(NW)]
    b_wsl = bufs(NW)
    MIXW = 14896
    mix = sb("mix", [128, MIXW])

    banks = [nc.alloc_psum_tensor("pb%d" % i, [128, 512], F32) for i in range(8)]
    b_bank = bufs(8)

    with ExitStack() as st:
        P = Prog(nc, st)
        c_w = [P.chan() for _ in range(NW)]
        c_x = P.chan(); c_xs = [P.chan(), P.chan()]; c_o = P.chan(); c_m1 = P.chan(); c_m2 = P.chan(); c_m3 = P.chan(); c_pw = P.chan(); c_e = [P.chan(), P.chan()]
        b_chain = Buf()

        wbi = [0]

        def wb():
            i = wbi[0] % 4
            wbi[0] += 1
            return banks[i], b_bank[i]

        def carve(off, n, dt=F32, **kw):
            v = mix[:, off:off + n]
            if dt == BF16:
                v = v.bitcast(BF16)
            return v

        def mm(out, lhsT, rhs, start, stop, reads, writes):
            P.op("pe", lambda e: e.matmul(out, lhsT=lhsT, rhs=rhs, start=start, stop=stop), reads, writes)

        def tp(out, in_, reads, writes):
            P.op("pe", lambda e: e.transpose(out, in_, ident[:]), [reads, b_ident], writes)

        def tpb(out, in_, reads, writes):
            P.op("pe", lambda e: e.transpose(out, in_, identb[:]), reads, writes)

        def act(out, in_, func, reads, writes, scale=1.0, bias=0.0):
            P.op("act", lambda e: e.activation(out=out, in_=in_, func=func, bias=bias, scale=scale), reads, writes)

        def tt(eng, out, in0, in1, op, reads, writes):
            P.op(eng, lambda e: e.tensor_tensor(out=out, in0=in0, in1=in1, op=op), reads, writes)

        def ts(eng, out, in0, s1, s2, op0, op1, reads, writes):
            if s2 is None:
                P.op(eng, lambda e: e.tensor_scalar(out=out, in0=in0, scalar1=s1, scalar2=None, op0=op0), reads, writes)
            else:
                P.op(eng, lambda e: e.tensor_scalar(out=out, in0=in0, scalar1=s1, scalar2=s2, op0=op0, op1=op1), reads, writes)

        def stt(out, in0, scalar, in1, op0, op1, reads, writes):
            P.op("dve", lambda e: e.scalar_tensor_tensor(out=out, in0=in0, scalar=scalar, in1=in1, op0=op0, op1=op1), reads, writes)

        def cp(eng, out, in_, reads, writes):
            if eng == "act":
                P.op("act", lambda e: e.copy(out=out, in_=in_), reads, writes)
            else:
                P.op(eng, lambda e: e.tensor_copy(out=out, in_=in_), reads, writes)

        def recip(out, in_, reads, writes):
            P.op("dve", lambda e: e.reciprocal(out=out, in_=in_), reads, writes)

        def memset(eng, ap, val, writes):
            P.op(eng, lambda e: e.memset(ap, val), None, writes)

        def wpiece(w2d, c0):
            return w2d.rearrange("(kc p) n -> p kc n", p=128)[:, :, c0:c0 + 512]

        seq = []
        for l in layers:
            ei = l // 2
            for j in range(4):
                seq.append(("kv%d" % j, wpiece(w_mkv_d[l], 512 * j)))
            for t in range(NT):
                if l % 2 == 0 and "mix" in SUBS:
                    for j in (0, 1, 3, 4, 2):
                        seq.append(("win%d" % j, wpiece(w_in_d[ei], 512 * j)))
                    for j in range(2):
                        seq.append(("wout%d" % j, wpiece(w_out_d[ei], 512 * j)))
                elif "mix" in SUBS:
                    for j in range(6):
                        seq.append(("qkv%d" % j, wpiece(w_qkv_d[ei], 512 * j)))
                    for j in range(2):
                        seq.append(("wo%d" % j, wpiece(w_oatt_d[ei], 512 * j)))
                if "mem" in SUBS:
                    for j in range(2):
                        seq.append(("mq%d" % j, wpiece(w_mq_d[l], 512 * j)))
                    for j in range(2):
                        seq.append(("mo%d" % j, wpiece(w_mo_d[l], 512 * j)))
                if "mlp" in SUBS:
                    for j in range(8):
                        seq.append(("w1_%d" % j, wpiece(w_ff1_d[l], 512 * j)))
                    for h in range(2):
                        for r in range(4):
                            seq.append(("w2_%d_%d" % (h, r), wpiece(w_ff2_d[l][1024 * r:1024 * (r + 1), :], 512 * h)))
        rs = {"issued": 0, "used": 0}

        def ring_get(name):
            i = rs["used"]
            assert seq[i][0] == name, (seq[i][0], name)
            while rs["issued"] < min(i + NW, len(seq)):
                k = rs["issued"]
                P.dma("pool", c_w[k % NW], wsl[k % NW][:], seq[k][1], writes=[b_wsl[k % NW]])
                rs["issued"] += 1
            rs["used"] += 1
            return wsl[i % NW], b_wsl[i % NW]

        memset("pool", ident[:], 0.0, [b_ident])
        P.op("pool", lambda e: e.affine_select(out=ident[:], in_=ident[:], pattern=[[1, 128]], compare_op=ALU.not_equal,
                                               fill=1.0, base=0, channel_multiplier=-1), None, [b_ident])
        memset("pool", onesb[:], 1.0, [b_onesb])
        memset("pool", onesf[:], 1.0, [b_onesf])
        memset("pool", mask[:], 1.0, [b_mask])
        for hh in range(2):
            P.op("pool", lambda e, hh=hh: e.affine_select(out=mask[64 * hh:64 * hh + 64, :], in_=mask[64 * hh:64 * hh + 64, :],
                                                        pattern=[[1, 64]], compare_op=ALU.is_ge, fill=0.0, base=0,
                                                        channel_multiplier=-1), None, [b_mask])
        ptmp = carve(0, 256).rearrange("p (r c) -> p r c", c=128); b_ptmp = Buf()
        P.dma("sp", c_m1, ptmp, par_d.rearrange("(r p) c -> p r c", p=128), writes=[b_ptmp])
        P.dma("sp", c_m2, cst[:], cst_d, writes=[b_cst])
        bk, bb = wb()
        for r in range(2):
            tp(bk[:, r * 128:(r + 1) * 128], ptmp[:, r, :], [b_ptmp], [bb])
        cp("dve", pv[:], bk[:, 0:256], [bb], [b_pv])
        tmp = carve(256, 64); b_tmp = Buf()
        tt("dve", tmp[:, 0:4], pv[:, 212:216], pv[:, 208:212], ALU.subtract, [b_pv], [b_tmp])
        act(tmp[:, 4:8], tmp[:, 0:4], AF.Exp, [b_tmp], [b_tmp])
        act(tmp[:, 8:12], tmp[:, 0:4], AF.Exp, [b_tmp], [b_tmp], scale=-1.0)
        ts("dve", tmp[:, 4:12], tmp[:, 4:12], 1.0, None, ALU.add, None, [b_tmp], [b_tmp])
        recip(tmp[:, 4:12], tmp[:, 4:12], [b_tmp], [b_tmp])
        tt("dve", lbv[:, 0:4], tmp[:, 4:8], tmp[:, 4:8], ALU.subtract, [b_tmp], [b_lbv])
        tt("dve", tmp[:, 12:16], tmp[:, 4:8], tmp[:, 8:12], ALU.add, [b_tmp], [b_tmp])
        tt("dve", lbv[:, 4:8], tmp[:, 12:16], tmp[:, 4:8], ALU.subtract, [b_tmp], [b_lbv])
        ts("dve", lbv[:, 0:8], lbv[:, 0:8], 0.0, 1.0, ALU.max, ALU.min, [b_lbv], [b_lbv])
        ts("dve", lbv[:, 8:16], lbv[:, 0:8], -1.0, 1.0, ALU.mult, ALU.add, [b_lbv], [b_lbv])
        memf = carve(512, 2048).rearrange("p (m f) -> p m f", f=1024); b_memf = Buf()
        P.dma("sp", c_m3, memf, mem_d.rearrange("(m p) f -> p m f", p=128), writes=[b_memf])
        junk = big[:, 0:2, :].rearrange("p a b -> p (a b)")
        for mb in range(2):
            P.op("act", lambda e, mb=mb: e.activation(out=junk, in_=memf[:, mb, :], func=AF.Square,
                                                      accum_out=tmp[:, 16 + mb:17 + mb]), [b_memf], [b_big[0], b_big[1], b_tmp])
        act(tmp[:, 18:20], tmp[:, 16:18], AF.Ln, [b_tmp], [b_tmp], scale=1.0 / D, bias=EPS)
        act(tmp[:, 20:22], tmp[:, 18:20], AF.Exp, [b_tmp], [b_tmp], scale=-0.5)
        for mb in range(2):
            ts("dve", memf[:, mb, :], memf[:, mb, :], tmp[:, 20 + mb:21 + mb], None, ALU.mult, None, [b_memf, b_tmp], [b_memf])
        for c in range(8):
            bk, bb = wb()
            for mb in range(2):
                tp(bk[:, mb * 128:(mb + 1) * 128], memf[:, mb, c * 128:(c + 1) * 128], [b_memf], [bb])
            ts("dve", memT[:, c, :], bk[:, 0:256], pv[:, 192 + c:193 + c], None, ALU.mult, None, [bb, b_pv], [b_memT])

        def rstd_from(bk, bb, dim):
            act(rstd[:], bk[:], AF.Ln, [bb], [b_rstd], scale=1.0 / dim, bias=EPS)
            act(rstd[:], rstd[:], AF.Exp, [b_rstd], [b_rstd], scale=-0.5)

        def prenorm(l, j):
            for c in range(8):
                act(sq[:, c, :], X["t"][:, c, :], AF.Square, [X["b"][c]], [b_sq[c]])
            bk, bb = wb()
            for c in range(8):
                mm(bk[:], onesb[:], sq[:, c, :], c == 0, c == 7, [b_onesb, b_sq[c]], [bb])
            rstd_from(bk, bb, D)
            for c in range(8):
                g = pv[:, (l * 6 + j) * 8 + c:(l * 6 + j) * 8 + c + 1]
                stt(hT[:, c, :], X["t"][:, c, :], g, rstd[:], ALU.mult, ALU.mult, [X["b"][c], b_pv, b_rstd], [b_hT[c]])

        def post_evac(c, bk, bb):
            cp("act", yT[:, c, :], bk[:], [bb], [b_yT[c]])
            act(sq[:, c, :], bk[:], AF.Square, [bb], [b_sq[c]])

        def post_finish(l, j):
            bk, bb = wb()
            for c in range(8):
                mm(bk[:], onesb[:], sq[:, c, :], c == 0, c == 7, [b_onesb, b_sq[c]], [bb])
            rstd_from(bk, bb, D)
            for c in range(8):
                g = pv[:, (l * 6 + j) * 8 + c:(l * 6 + j) * 8 + c + 1]
                tt("dve", yT[:, c, :], yT[:, c, :], rstd[:], ALU.mult, [b_yT[c], b_rstd], [b_yT[c]])
                stt(X["t"][:, c, :], yT[:, c, :], g, X["t"][:, c, :], ALU.mult, ALU.add, [b_yT[c], b_pv, X["b"][c]], [X["b"][c]])

        def proj_fm(names, nout, src, b_src, evac):
            for oc in range(nout):
                if oc % 4 == 0:
                    pc, pb = ring_get(names[oc // 4])
                bk, bb = wb()
                for kc in range(8):
                    mm(bk[:], pc[:, kc, (oc % 4) * 128:(oc % 4 + 1) * 128], src[:, kc, :], kc == 0, kc == 7,
                       [pb, b_src[kc]], [bb])
                evac(oc, bk, bb)

        def proj_tm(name, src, b_src, evac):
            pc, pb = ring_get(name)
            for tb in range(4):
                bk, bb = wb()
                for kc in range(8):
                    mm(bk[:], src[:, kc, tb * 128:(tb + 1) * 128], pc[:, kc, :], kc == 0, kc == 7, [pb, b_src[kc]], [bb])
                evac(tb, bk, bb)

        ytok = yT[:].rearrange("p (tb h) f -> p tb (h f)", h=2)

        def load_x(l, t, first, par):
            if first:
                src = x_d[t * T:(t + 1) * T, :].rearrange("(tb p) f -> p tb f", p=128)
                P.dma("sp", c_x, ytok, src, writes=b_yT)
                for c in range(8):
                    bk, bb = wb()
                    for tb in range(4):
                        tp(bk[:, tb * 128:(tb + 1) * 128], ytok[:, tb, c * 128:(c + 1) * 128], [b_yT[2 * tb], b_yT[2 * tb + 1]], [bb])
                    cp("act" if c % 2 else "dve", xTs[par][:, c, :], bk[:], [bb], [b_xTs[par][c]])
            else:
                P.dma("sp", c_xs[par], xTs[par][:], xs_d[:, :, t * T:(t + 1) * T].rearrange("c p s -> p c s"), reads=[b_xs[t]], writes=b_xTs[par])

        def store_x(l, t, last):
            if last:
                for tb in range(4):
                    for h in range(2):
                        bk, bb = wb()
                        for cc in range(4):
                            c = 4 * h + cc
                            tp(bk[:, cc * 128:(cc + 1) * 128], X["t"][:, c, tb * 128:(tb + 1) * 128], [X["b"][c]], [bb])
                        cp("act" if h else "dve", ytok[:, tb, h * 512:(h + 1) * 512], bk[:], [bb], [b_yT[2 * tb + h]])
                dst = y_d[t * T:(t + 1) * T, :].rearrange("(tb p) f -> p tb f", p=128)
                return P.dma("sp", c_o, dst, ytok, reads=b_yT, writes=[b_chain])
            else:
                return P.dma("sp", c_o, xs_d[:, :, t * T:(t + 1) * T].rearrange("c p s -> p c s"), X["t"][:], reads=X["b"], writes=[b_xs[t], b_chain])

        b_xs = bufs(NT)

        def mem_attn(l):
            prenorm(l, 2)
            qT = big[:, 0:8, :]; b_q = b_big[0:8]
            oT = big[:, 8:16, :]; b_o = b_big[8:16]
            proj_fm(["mq0", "mq1"], 8, hT, b_hT,
                    lambda oc, bk, bb: cp("act" if oc % 2 else "dve", qT[:, oc, :], bk[:], [bb], [b_q[oc]]))
            for hd in range(4):
                pts = []
                for mb in range(2):
                    bk, bb = wb()
                    for i in range(2):
                        c = 2 * hd + i
                        mm(bk[:], kmT[:, c, mb * 128:(mb + 1) * 128], qT[:, c, :], i == 0, i == 1, [b_kmT[c], b_q[c]], [bb])
                    pt = big[:, 16 + mb, :]; bp = b_big[16 + mb]
                    act(pt, bk[:], AF.Exp, [bb], [bp], scale=1.0 / 16.0)
                    pts.append((pt, bp))
                dk, db = wb()
                for mb in range(2):
                    mm(dk[:], onesb[:], pts[mb][0], mb == 0, mb == 1, [b_onesb, pts[mb][1]], [db])
                rd = big[:, 18:20, :].rearrange("p a b -> p (a b)").bitcast(F32); brd = [b_big[18], b_big[19]]
                recip(rd, dk[:], [db], brd)
                for dc in range(2):
                    c = 2 * hd + dc
                    bk, bb = wb()
                    for mb in range(2):
                        mm(bk[:], vm[:, mb, c * 128:(c + 1) * 128], pts[mb][0], mb == 0, mb == 1, [b_vm[mb], pts[mb][1]], [bb])
                    tt("dve", oT[:, c, :], bk[:], rd, ALU.mult, [bb, brd], [b_o[c]])
            proj_fm(["mo0", "mo1"], 8, oT, b_o, post_evac)
            post_finish(l, 3)

        def mlp(l):
            prenorm(l, 4)

            def ev1(hc, bk, bb):
                act(big[:, hc, :], bk[:], AF.Relu, [bb], [b_big[hc]])
                tt("dve", big[:, hc, :], big[:, hc, :], big[:, hc, :], ALU.mult, [b_big[hc]], [b_big[hc]])
            proj_fm(["w1_%d" % j for j in range(8)], 32, hT, b_hT, ev1)
            for h in range(2):
                ab = (4, 5, 6, 7) if h == 0 else (0, 1, 2, 3)
                for r in range(4):
                    pc, pb = ring_get("w2_%d_%d" % (h, r))
                    for i in range(4):
                        for kk in range(8):
                            hc = 8 * r + kk
                            mm(banks[ab[i]][:], pc[:, kk, i * 128:(i + 1) * 128], big[:, hc, :],
                               r == 0 and kk == 0, r == 3 and kk == 7, [pb, b_big[hc]], [b_bank[ab[i]]])
                for i in range(4):
                    post_evac(4 * h + i, banks[ab[i]], b_bank[ab[i]])
            post_finish(l, 5)

        o = [0]

        def take(n):
            a = o[0]
            o[0] += n
            return a
        E_UB = take(4 * 528); E_QS = take(1024); E_KS = take(1024); E_SG = take(1024); E_VT = take(1024)
        E_TMP = [take(512) for _ in range(7)]
        E_KT = [take(128) for _ in range(2)]
        E_G = take(1024); E_SB = take(512); E_S = take(512); E_AM = [take(256) for _ in range(2)]
        E_PT = [take(528) for _ in range(3)]; E_FV = take(64); E_PW = take(256); E_FVH = take(96)
        assert o[0] <= MIXW, o[0]
        ub = carve(E_UB, 4 * 528).rearrange("p (g t) -> p g t", t=528); b_ub = bufs(4)
        qs = carve(E_QS, 1024, BF16).rearrange("p (h t) -> p h t", t=512); b_qs = bufs(4)
        ks = carve(E_KS, 1024, BF16).rearrange("p (h t) -> p h t", t=512); b_ks = bufs(4)
        sg = carve(E_SG, 1024, BF16).rearrange("p (h t) -> p h t", t=512); b_sg = bufs(4)
        vt = carve(E_VT, 1024, BF16).rearrange("p (b f) -> p b f", f=512); b_vt = bufs(4)
        tmpf = [carve(a, 512) for a in E_TMP]; b_tmpf = bufs(7)
        kst = [carve(a, 128, BF16).rearrange("p (b d) -> p b d", d=128) for a in E_KT]; b_kst = bufs(2)
        Gs = carve(E_G, 1024).rearrange("p (j e) -> p j e", e=128); b_Gs = Buf()
        Sb = carve(E_SB, 512, BF16).rearrange("p (j e) -> p j e", e=128); b_Sb = Buf()
        Sst = carve(E_S, 512).rearrange("p (h e) -> p h e", e=128); b_S = bufs(4)
        Am = [carve(a, 256, BF16) for a in E_AM]; b_Am = bufs(2)
        ptm = [carve(a, 528) for a in E_PT]; b_ptm = bufs(3)
        fv = carve(E_FV, 64); b_fv = Buf()
        poolw = carve(E_PW, 256, BF16).rearrange("p (g d) -> p g d", d=128); b_poolw = Buf()
        fvh = carve(E_FVH, 96).rearrange("p (h a) -> p h a", a=24); b_fvh = bufs(4)

        def even_layer_init(l):
            ei = l // 2
            if MIXSTOP < 0.1:
                return
            pwst = tmpf[0].rearrange("p (g d) -> p g d", d=128)
            P.dma("sp", c_pw, pwst, poolw_d[ei].rearrange("g c d -> c g d"), writes=[b_tmpf[0]])
            cp("dve", poolw, pwst, [b_tmpf[0]], [b_poolw])
            for hd in range(4):
                memset("dve", Sst[:, hd, :], 0.0, [b_S[hd]])
            for g in range(4):
                memset("dve", ub[:, g, 0:16], 0.0, [b_ub[g]])
            for i in range(2):
                memset("dve", Am[i], 0.0, [b_Am[i]])

        def even_mixer(l, t):
            ei = l // 2
            prenorm(l, 0)
            def gate(th, name, f):
                if MIXSTOP >= th:
                    f()
                else:
                    ring_get(name)
            gate(0.2, "win0", lambda: proj_fm(["win0"], 4, hT, b_hT,
                    lambda oc, bk, bb: cp("act", ub[:, oc, 16:528], bk[:], [bb], [b_ub[oc]])))
            gate(0.3, "win1", lambda: proj_fm(["win1"], 4, hT, b_hT,
                    lambda oc, bk, bb: cp("dve", qs[:, oc, :], bk[:], [bb], [b_qs[oc]])))

            def f32v(c0):
                return big[:, c0:c0 + 2, :].rearrange("p a b -> p (a b)").bitcast(F32), [b_big[c0], b_big[c0 + 1]]
            EB = [(tmpf[0], [b_tmpf[0]]), f32v(17), f32v(19), f32v(21)]
            TS = [[(tmpf[1], [b_tmpf[1]]), (tmpf[2], [b_tmpf[2]]), (tmpf[3], [b_tmpf[3]])],
                  [(tmpf[4], [b_tmpf[4]]), (tmpf[5], [b_tmpf[5]]), (tmpf[6], [b_tmpf[6]])]]
            FV = [(fv, [b_fv]), (big[:, 23, :].bitcast(F32), [b_big[23]])]

            def run_interleaved(gens):
                gens = list(gens)
                while gens:
                    for g_ in list(gens):
                        try:
                            next(g_)
                        except StopIteration:
                            gens.remove(g_)

            def ev_fz(hd, bk, bb):
                act(EB[hd][0], bk[:], AF.Sigmoid, [bb], EB[hd][1])

            def fz_chain(hd):
                par = hd % 2
                E_, bE = EB[hd]
                (f_, bf_), (k_, bk_), (ex_, bex) = TS[par]
                fvx, bfv = FV[par]
                lf_, blf = E_, bE
                B_, bB = f_, bf_
                bq_, bbq = E_, bE
                lb = lbv[:, ei * 4 + hd:ei * 4 + hd + 1]
                oml = lbv[:, 8 + ei * 4 + hd:8 + ei * 4 + hd + 1]
                P.op("act", lambda e: e.activation(out=f_, in_=E_, func=AF.Identity, scale=oml, bias=lb), [bE, b_lbv], [bf_])
                yield
                act(k_, f_, AF.Identity, [bf_], [bk_], scale=-1.0, bias=1.0)
                act(lf_, f_, AF.Ln, [bf_], [blf])
                yield
                P.op("dve", lambda e: e.tensor_tensor_scan(out=B_, data0=onesf[:], data1=lf_, initial=0.0,
                                                           op0=ALU.mult, op1=ALU.add), [b_onesf, blf], [bB])
                yield
                B3 = B_.rearrange("p (j s) -> p j s", s=64)
                cmid = B3[:, :, 31:32]
                bend = B3[:, :, 63:64]
                memset("dve", fvx[:, 24:25], 0.0, [bfv])
                cp("dve", fvx[:, 25:32].rearrange("p (j o) -> p j o", o=1), B3[:, 0:7, 63:64], [bB], [bfv])
                bs3 = fvx[:, 24:32].rearrange("p (j o) -> p j o", o=1)
                tt("dve", fvx[:, 0:8].rearrange("p (j o) -> p j o", o=1), cmid, bs3, ALU.subtract, [bB, bfv], [bfv])
                tt("dve", fvx[:, 8:16].rearrange("p (j o) -> p j o", o=1), bend, bs3, ALU.subtract, [bB, bfv], [bfv])
                tt("dve", fvx[:, 16:24].rearrange("p (j o) -> p j o", o=1), bend, cmid, ALU.subtract, [bB], [bfv])
                yield
                act(fvx[:, 32:56], fvx[:, 0:24], AF.Exp, [bfv], [bfv])
                tt("dve", bq_.rearrange("p (j s) -> p j s", s=64), B3, cmid.to_broadcast([128, 8, 64]), ALU.subtract, [bB], [bbq])
                yield
                act(ex_, bq_, AF.Exp, [bbq], [bex])
                cp("dve", fvh[:, hd, :], fvx[:, 32:56], [bfv], [b_fvh[hd]])
                yield
                tt("dve", qs[:, hd, :], qs[:, hd, :], ex_, ALU.mult, [b_qs[hd], bex], [b_qs[hd]])
                yield
                act(ex_, bq_, AF.Exp, [bbq], [bex], scale=-1.0)
                yield
                tt("dve", ks[:, hd, :], k_, ex_, ALU.mult, [bk_, bex], [b_ks[hd]])
                yield
            gate(0.5, "win3", lambda: proj_tm("win3", hT, b_hT, lambda tb, bk, bb: cp("act", vt[:, tb, :], bk[:], [bb], [b_vt[tb]])))

            def ev_g(hd, bk, bb):
                act(sg[:, hd, :], bk[:], AF.Silu, [bb], [b_sg[hd]])
            gate(1, "win4", lambda: proj_fm(["win4"], 4, hT, b_hT, ev_g))
            proj_fm(["win2"], 4, hT, b_hT, ev_fz)
            run_interleaved([fz_chain(0), fz_chain(1)])
            run_interleaved([fz_chain(2), fz_chain(3)])

            mixin = big[:, 0:8, :]; b_mi = b_big[0:8]
            for g in range(4 if MIXSTOP >= 2 else 0):
                w = 2 << g
                cur = ub[:, g, :]; bcur = b_ub[g]
                lo = 0
                sh = 1
                k = 0
                while sh < w:
                    nxt = ptm[k % 2]; bn = b_ptm[k % 2]
                    lo2 = lo + sh
                    tt("dve", nxt[:, lo2:528], cur[:, lo2:528], cur[:, lo2 - sh:528 - sh], ALU.add, [bcur], [bn])
                    cur, bcur, lo, sh, k = nxt, bn, lo2, sh * 2, k + 1
                pl = hT[:, g, :]
                stt(pl, cur[:, 16:528], 1.0 / w, ub[:, g, 16:528], ALU.mult, ALU.subtract, [bcur, b_ub[g]], [b_hT[g]])
                if t == 0:
                    fx = ptm[2][:, 0:16]
                    tt("dve", fx, cur[:, 16:32], cst[:, g * 16:(g + 1) * 16], ALU.mult, [bcur, b_cst], [b_ptm[2]])
                    tt("dve", pl[:, 0:16], fx, ub[:, g, 16:32], ALU.subtract, [b_ptm[2], b_ub[g]], [b_hT[g]])
                cp("dve", ub[:, g, 0:16], ub[:, g, 512:528], [b_ub[g]], [b_ub[g]])
                bk, bb = wb()
                mm(bk[:], poolw[:, g, :], pl, True, True, [b_poolw, b_hT[g]], [bb])
                P.op("act", lambda e, g=g, bk=bk: e.activation(out=mixin[:, g, :], in_=bk[:], func=AF.Copy, scale=pv[:, 200 + ei * 4 + g:201 + ei * 4 + g]),
                     [bb, b_pv], [b_mi[g]])
            kst2 = [big[:, 8, 0:256].rearrange("p (b d) -> p b d", d=128), big[:, 8, 256:512].rearrange("p (b d) -> p b d", d=128)]
            Gs2 = big[:, 9:13, :].rearrange("p a b -> p (a b)").bitcast(F32).rearrange("p (j e) -> p j e", e=128)
            Sb2 = big[:, 13:15, :].rearrange("p a b -> p (a b)").rearrange("p (j e) -> p j e", e=128)
            oc2, boc2 = f32v(15)
            RES = [dict(kst=kst, bkst=[[b_kst[0]], [b_kst[1]]], Gs=Gs, bGs=[b_Gs], Sb=Sb, bSb=[b_Sb], oc=tmpf[2], boc=[b_tmpf[2]], banks=(0, 1)),
                   dict(kst=kst2, bkst=[[b_big[8]], [b_big[8]]], Gs=Gs2, bGs=b_big[9:13], Sb=Sb2, bSb=b_big[13:15], oc=oc2, boc=boc2, banks=(2, 3))]

            def hgrn_head(hd):
                R_ = RES[hd % 2]
                cnt = [0]

                def hb():
                    i = R_["banks"][cnt[0] % 2]
                    cnt[0] += 1
                    return banks[i], b_bank[i]
                kst_, bkst_, Gs_, bGs_, Sb_, bSb_, oc_, boc = R_["kst"], R_["bkst"], R_["Gs"], R_["bGs"], R_["Sb"], R_["bSb"], R_["oc"], R_["boc"]
                F1 = fvh[:, hd, 0:8]; F2 = fvh[:, hd, 8:16]; F3 = fvh[:, hd, 16:24]
                ak, ab = hb()
                for j in range(8):
                    r0 = 64 * (j % 2)
                    mm(ak[r0:r0 + 64, j * 64:(j + 1) * 64], ks[:, hd, j * 64:(j + 1) * 64], qs[:, hd, j * 64:(j + 1) * 64], True, True,
                       [b_ks[hd], b_qs[hd]], [ab])
                am = Am[hd % 2]; bam = b_Am[hd % 2]
                for r in range(2):
                    r0 = 64 * r
                    src = ak[r0:r0 + 64, :].rearrange("p (j two s) -> p j two s", two=2, s=64)[:, :, r, :]
                    dst = am[r0:r0 + 64, :].rearrange("p (j two s) -> p j two s", two=2, s=64)[:, :, r, :]
                    tt("dve", dst, src, mask[r0:r0 + 64, :].rearrange("p (o s) -> p o s", o=1).to_broadcast([64, 4, 64]), ALU.mult,
                       [ab, b_mask], [bam])
                yield
                for half in range(2):
                    bk, bb = hb()
                    bkb = bk[:].bitcast(BF16)
                    for q in range(2):
                        blk = 2 * half + q
                        tpb(bkb[:, q * 128:(q + 1) * 128], ks[:, hd, blk * 128:(blk + 1) * 128], [b_ks[hd], b_identb], [bb])
                    cp("act", kst_[half].rearrange("p b d -> p (b d)"), bkb[:, 0:256], [bb], bkst_[half])
                    yield
                Gs4 = Gs_.rearrange("p (jj two) e -> p jj two e", two=2)
                for r in range(2):
                    r0 = 64 * r
                    bk, bb = hb()
                    for jj in range(4):
                        blk = jj
                        mm(bk[:, jj * 128:(jj + 1) * 128], kst_[blk // 2][r0:r0 + 64, blk % 2, :], vt[r0:r0 + 64, blk, hd * 128:(hd + 1) * 128],
                           True, True, [bkst_[blk // 2], b_vt[blk]], [bb])
                    f3 = F3.rearrange("p (jj two) -> p jj two", two=2)[:, :, r:r + 1]
                    tt("dve", Gs4[:, :, r, :], bk[:].rearrange("p (j e) -> p j e", e=128),
                       f3.to_broadcast([128, 4, 128]), ALU.mult, [bb, b_fvh[hd]], bGs_)
                    yield
                for j in range(8):
                    P.op("act", lambda e, j=j: e.activation(out=Sb_[:, j, :], in_=Sst[:, hd, :], func=AF.Copy, scale=F1[:, j:j + 1]),
                         [b_S[hd], b_fvh[hd]], bSb_)
                    stt(Sst[:, hd, :], Sst[:, hd, :], F2[:, j:j + 1], Gs_[:, j, :], ALU.mult, ALU.add, [b_S[hd], b_fvh[hd], bGs_], [b_S[hd]])
                    if j % 2 == 1:
                        yield
                ok, ob = hb()
                for blk in range(4):
                    mm(ok[:, blk * 128:(blk + 1) * 128], vt[:, blk, hd * 128:(hd + 1) * 128], am[:, blk * 128:(blk + 1) * 128], True, False,
                       [b_vt[blk], bam], [ob])
                    for r in range(2):
                        j = 2 * blk + r
                        mm(ok[:, j * 64:(j + 1) * 64], Sb_[:, j, :], qs[:, hd, j * 64:(j + 1) * 64], False, r == 1, [bSb_, b_qs[hd]], [ob])
                osq = hT[:, 4 + hd, :]
                cp("act", oc_, ok[:], [ob], boc)
                yield
                act(osq, oc_, AF.Square, boc, [b_hT[4 + hd]])
                sk, sbb = hb()
                mm(sk[:], onesb[:], osq, True, True, [b_onesb, b_hT[4 + hd]], [sbb])
                yield
                rstd_from(sk, sbb, 128)
                stt(oc_, oc_, pv[:, 216 + ei:217 + ei], rstd[:], ALU.mult, ALU.mult, [boc, b_pv, b_rstd], boc)
                yield
                tt("dve", mixin[:, 4 + hd, :], oc_, sg[:, hd, :], ALU.mult, [boc, b_sg[hd]], [b_mi[4 + hd]])
                yield
            run_interleaved([hgrn_head(0), hgrn_head(1)])
            run_interleaved([hgrn_head(2), hgrn_head(3)])
            if MIXSTOP < 6:
                for c in range(8):
                    memset("dve", mixin[:, c, :], 0.0, [b_mi[c]])
            if MIXPART == "pool":
                for c in range(4, 8):
                    memset("dve", mixin[:, c, :], 0.0, [b_mi[c]])
            if MIXPART == "rec":
                for c in range(4):
                    memset("dve", mixin[:, c, :], 0.0, [b_mi[c]])
            proj_fm(["wout0", "wout1"], 8, mixin, b_mi, post_evac)
            post_finish(l, 1)

        antiJ = sb("antiJ", [128, 128]); b_antiJ = Buf()
        memset("pool", antiJ[:], 0.0, [b_antiJ])
        P.op("pool", lambda e: e.affine_select(out=antiJ[:], in_=antiJ[:], pattern=[[1, 128]], compare_op=ALU.not_equal,
                                               fill=1.0, base=-127, channel_multiplier=1), None, [b_antiJ])
        identb = sb("identb", [128, 128], BF16); b_identb = Buf()
        cp("dve", identb[:], ident[:], [b_ident], [b_identb])

        O_KT = 0; O_VT = 4096; O_E = 8192; O_BR = 8192 + 5120
        kT = carve(O_KT, 4096, BF16).rearrange("p (s c t) -> p s c t", s=2, c=8); b_kT = [bufs(8), bufs(8)]
        vtk = carve(O_VT, 4096, BF16).rearrange("p (s b f) -> p s b f", s=2, b=4); b_vtk = [bufs(4), bufs(4)]
        Et = carve(O_E, 5120, BF16).rearrange("p (h m) -> p h m", m=640); b_E = bufs(16)
        braw = [carve(O_BR + 640 * i, 640) for i in range(2)]; b_braw = bufs(2)
        assert O_BR + 1280 <= MIXW

        def odd_layer_init(l):
            oi = l // 2
            for hh in range(16):
                br = braw[hh % 2]; bbr = b_braw[hh % 2]
                src = bass.AP(relx_t, (oi * 16 + hh) * 768, [[1, 128], [1, 640]])
                P.dma("sp", c_e[hh % 2], br, src, writes=[bbr])
                bA, bAb = wb()
                mm(bA[:], antiJ[:], br[:, 0:512], True, True, [b_antiJ, bbr], [bAb])
                bB, bBb = wb()
                mm(bB[:, 0:128], antiJ[:], br[:, 512:640], True, True, [b_antiJ, bbr], [bBb])
                act(Et[:, hh, 0:512], bA[:], AF.Exp, [bAb], [b_E[hh]])
                act(Et[:, hh, 512:640], bB[:, 0:128], AF.Exp, [bBb], [b_E[hh]])
                memset("dve", Et[64:128, hh, 0:64], 0.0, [b_E[hh]])
                memset("dve", Et[0:64, hh, 576:640], 0.0, [b_E[hh]])

        def odd_mixer(l, t):
            slot = t % 2
            prenorm(l, 0)
            qT = big[:, 0:8, :]; b_q = b_big[0:8]
            oT = big[:, 8:16, :]; b_o = b_big[8:16]
            proj_fm(["qkv0", "qkv1"], 8, hT, b_hT,
                    lambda oc, bk, bb: cp("act" if oc % 2 else "dve", qT[:, oc, :], bk[:], [bb], [b_q[oc]]))
            proj_fm(["qkv2", "qkv3"], 8, hT, b_hT,
                    lambda oc, bk, bb: cp("act" if oc % 2 else "dve", kT[:, slot, oc, :], bk[:], [bb], [b_kT[slot][oc]]))
            for h in range(2):
                proj_tm("qkv%d" % (4 + h), hT, b_hT,
                        lambda tb, bk, bb, h=h: cp("act" if tb % 2 else "dve", vtk[:, slot, tb, h * 512:(h + 1) * 512], bk[:], [bb], [b_vtk[slot][tb]]))
            items = [(qb, hp, r) for qb in range(4) for hp in range(8) for r in range(2)]

            def emit_scores(it):
                qb, hp, r = it
                G = 4 * t + qb
                nk = min(G, 4) + 1
                r0 = 64 * r
                sA, sAb = wb()
                sB = sBb = None
                if nk == 5:
                    sB, sBb = wb()
                kinfo = []
                for k in range(nk):
                    g = G - k
                    gs = (g // 4) % 2; gl = g % 4
                    kinfo.append((gs, gl))
                    bk_, bb_ = (sA, sAb) if k < 4 else (sB, sBb)
                    col = (k % 4) * 128
                    mm(bk_[:, col:col + 128], kT[r0:r0 + 64, gs, hp, gl * 128:(gl + 1) * 128], qT[r0:r0 + 64, hp, qb * 128:(qb + 1) * 128],
                       True, True, [b_kT[gs][hp], b_q[hp]], [bb_])
                return (it, nk, sA, sAb, sB, sBb, kinfo)

            def emit_rest(ctx, hi):
                (qb, hp, r), nk, sA, sAb, sB, sBb, kinfo = ctx
                hh = 2 * hp + r
                r0 = 64 * r
                pt = pT2[:, hi % 2, :]; bp = b_pT2[hi % 2]
                na = min(nk, 4) * 128
                act(pt[:, 0:na], sA[:, 0:na], AF.Exp, [sAb], [bp], scale=0.125)
                if nk == 5:
                    act(pt[:, 512:640], sB[:, 0:128], AF.Exp, [sBb], [bp], scale=0.125)
                tt("dve", pt[:, 0:nk * 128], pt[:, 0:nk * 128], Et[:, hh, 0:nk * 128], ALU.mult, [bp, b_E[hh]], [bp])
                po = banks[4 + hp // 4]; pob = b_bank[4 + hp // 4]
                pd = banks[6 + hp // 4]; pdb = b_bank[6 + hp // 4]
                cs = (hp % 4) * 128
                for k in range(nk):
                    gs, gl = kinfo[k]
                    mm(po[r0:r0 + 64, cs:cs + 128], vtk[:, gs, gl, hh * 64:(hh + 1) * 64], pt[:, k * 128:(k + 1) * 128], k == 0, k == nk - 1,
                       [b_vtk[gs][gl], bp], [pob])
                for k in range(nk):
                    mm(pd[r0:r0 + 64, cs:cs + 128], onesb[:, 0:64], pt[:, k * 128:(k + 1) * 128], k == 0, k == nk - 1,
                       [b_onesb, bp], [pdb])
                if hp == 7 and r == 1:
                    for h in range(2):
                        rd = rstd[:]
                        recip(rd, banks[6 + h][:], [b_bank[6 + h]], [b_rstd])
                        tt("dve", oT[:, 4 * h:4 * h + 4, qb * 128:(qb + 1) * 128], banks[4 + h][:].rearrange("p (c q) -> p c q", q=128),
                           rd.rearrange("p (c q) -> p c q", q=128), ALU.mult, [b_bank[4 + h], b_rstd], b_o[4 * h:4 * h + 4])

            ctx = emit_scores(items[0])
            for i in range(len(items)):
                nxt = emit_scores(items[i + 1]) if i + 1 < len(items) else None
                emit_rest(ctx, i)
                ctx = nxt
            proj_fm(["wo0", "wo1"], 8, oT, b_o, post_evac)
            post_finish(l, 1)

        out_toks = []
        loaded = set()
        for li, l in enumerate(layers):
            first = li == 0
            last = li == len(layers) - 1
            P.barrier()
            for oc in range(8):
                if oc % 4 == 0:
                    pc, pb = ring_get("kv%d" % (oc // 4))
                bk, bb = wb()
                for kc in range(8):
                    mm(bk[:, 0:256], pc[:, kc, (oc % 4) * 128:(oc % 4 + 1) * 128], memT[:, kc, :], kc == 0, kc == 7, [pb, b_memT], [bb])
                cp("act", kmT[:, oc, :], bk[:, 0:256], [bb], [b_kmT[oc]])
            for h in range(2):
                pc, pb = ring_get("kv%d" % (2 + h))
                for mb in range(2):
                    bk, bb = wb()
                    for kc in range(8):
                        mm(bk[:], memT[:, kc, mb * 128:(mb + 1) * 128], pc[:, kc, :], kc == 0, kc == 7, [pb, b_memT], [bb])
                    cp("dve", vm[:, mb, h * 512:(h + 1) * 512], bk[:], [bb], [b_vm[mb]])
            if l % 2 == 0:
                even_layer_init(l)
            else:
                odd_layer_init(l)
            for t in range(NT):
                g = li * NT + t
                par = g % 2
                if g not in loaded:
                    load_x(l, t, first, par)
                    loaded.add(g)
                if g + 1 < len(layers) * NT and (g + 1) // NT > 0 and (g + 1) not in loaded:
                    l2 = layers[(g + 1) // NT]; t2 = (g + 1) % NT
                    if (g + 1) // NT == li or t2 == 0 and False:
                        load_x(l2, t2, False, 1 - par)
                        loaded.add(g + 1)
                X["t"] = xTs[par]; X["b"] = b_xTs[par]
                if "mix" in SUBS:
                    if l % 2 == 0:
                        even_mixer(l, t)
                    else:
                        odd_mixer(l, t)
                if "mem" in SUBS:
                    mem_attn(l)
                if "mlp" in SUBS:
                    mlp(l)
                tok = store_x(l, t, last)
                if last:
                    out_toks.append(tok)
        P.ops["sp"].append(Op(None, [out_toks[-1]]))
        stats = P.emit()
    return nc, stats


def _host_prep(inputs):
    f = lambda a: np.ascontiguousarray(np.asarray(a, dtype=np.float32))
    ng = f(inputs["norm_gains"]).reshape(192, 128)
    params = np.concatenate([
        ng,
        f(inputs["mem_norm_gain"]).reshape(8, 128),
        f(inputs["pool_scale"]).reshape(8, 128),
        f(inputs["rec_lb_logits"]).reshape(8, 128),
        f(inputs["rec_out_gain"]).reshape(2, 128),
        np.zeros((38, 128), np.float32)], axis=0)
    consts = np.zeros((128, 64), np.float32)
    for g, w in enumerate((2, 4, 8, 16)):
        consts[:, g * 16:(g + 1) * 16] = 1.0 / np.minimum(np.arange(1, 17), w)
    idx = np.clip(np.arange(768) - 127, -256, 256) + 256
    relx = f(inputs["rel_bias"])[:, :, idx]
    shared = {"params": params, "consts": consts, "relx": np.ascontiguousarray(relx)}
    for k in ("w_in_ab", "pool_w", "w_out_ab", "w_qkv", "w_o_att", "w_mem_q", "w_mem_kv", "w_mem_o", "w_ff1", "w_ff2"):
        shared[k] = f(inputs[k])
    return shared


_CACHE = {}


def kernel(**inputs):
    x = np.asarray(inputs["x"], dtype=np.float32)
    mem = np.asarray(inputs["mem"], dtype=np.float32)
    B, S, _ = x.shape
    shared = _host_prep(inputs)
    key = (S,)
    if key not in _CACHE:
        _CACHE[key] = build(S, [0, 1, 2, 3])[0]
    nc = _CACHE[key]
    in_maps = []
    for b in range(B):
        m = dict(shared)
        m["x"] = np.ascontiguousarray(x[b])
        m["mem"] = np.ascontiguousarray(mem[b])
        in_maps.append(m)
    res = run_bass_kernel_spmd(nc, in_maps, core_ids=list(range(B)))
    return np.stack([res.results[b]["y"] for b in range(B)], axis=0).astype(np.float32)
```

```python
import sys
from contextlib import ExitStack
import numpy as np
import concourse.bass as bass
import concourse.mybir as mybir
from concourse.bass_utils import run_bass_kernel_spmd

F32 = mybir.dt.float32
BF16 = mybir.dt.bfloat16
AF = mybir.ActivationFunctionType
ALU = mybir.AluOpType

D = 1024
T = 512
NW = 4
EPS = 1e-6
ENGS = ("pe", "act", "dve", "pool", "sp")
SUBS = ("mix", "mem", "mlp")
MIXPART = "all"
MIXSTOP = 6


class Buf:
    __slots__ = ("w", "r")

    def __init__(self):
        self.w = None
        self.r = {}


def bufs(n):
    return [Buf() for _ in range(n)]


class Chan:
    __slots__ = ("sem", "count")

    def __init__(self, sem):
        self.sem = sem
        self.count = 0


class Op:
    __slots__ = ("fn", "deps", "signal", "cnt", "dma", "src")

    def __init__(self, fn, deps, dma=None):
        try:
            f = sys._getframe(2)
            self.src = (f.f_lineno, f.f_back.f_lineno if f.f_back else 0, f.f_back.f_back.f_lineno if f.f_back and f.f_back.f_back else 0)
        except ValueError:
            self.src = None
        self.fn = fn
        self.deps = deps
        self.signal = False
        self.cnt = 0
        self.dma = dma


def _flat(x, out):
    if x is None:
        return out
    if isinstance(x, Buf):
        out.append(x)
        return out
    for y in x:
        _flat(y, out)
    return out


class Prog:
    def __init__(self, nc, stack):
        self.nc = nc
        self.ops = {e: [] for e in ENGS}
        self.sem = {e: stack.enter_context(nc.semaphore("s_" + e)) for e in ENGS}
        self.stack = stack
        self.chans = []

    def chan(self):
        c = Chan(self.stack.enter_context(self.nc.semaphore("c%d" % len(self.chans))))
        self.chans.append(c)
        return c

    def _collect(self, eng, rd, wr):
        deps = {}
        for b in rd:
            if b.w is not None:
                deps[b.w[1:]] = b.w
        for b in wr:
            if b.w is not None:
                deps[b.w[1:]] = b.w
            for t in b.r.values():
                deps[t[1:]] = t
        out = []
        for t in deps.values():
            if t[0] == "e":
                if t[1] == eng and eng == "pe":
                    continue
                self.ops[t[1]][t[2]].signal = True
            out.append(t)
        return out

    def _mark(self, tok, rd, wr):
        key = tok[1]
        for b in rd:
            b.r[key] = tok
        for b in wr:
            b.w = tok
            b.r = {}

    def op(self, eng, fn, reads=None, writes=None):
        rd = _flat(reads, [])
        wr = _flat(writes, [])
        deps = self._collect(eng, rd, wr)
        seq = len(self.ops[eng])
        self.ops[eng].append(Op(fn, deps))
        self._mark(("e", eng, seq), rd, wr)

    def dma(self, q, chan, out_ap, in_ap, reads=None, writes=None):
        rd = _flat(reads, [])
        wr = _flat(writes, [])
        deps = self._collect(q, rd, wr)
        chan.count += 16
        tok = ("d", chan, chan.count)
        self.ops[q].append(Op(lambda e: e.dma_start(out=out_ap, in_=in_ap), deps, dma=(chan, chan.count)))
        self._mark(tok, rd, wr)
        return tok

    def barrier(self):
        last = {}
        for e in ENGS:
            if self.ops[e]:
                for i in range(len(self.ops[e]) - 1, -1, -1):
                    if self.ops[e][i].fn is not None and self.ops[e][i].dma is None:
                        last[e] = i
                        break
        for e in ENGS:
            deps = []
            for f, i in last.items():
                if f != e:
                    self.ops[f][i].signal = True
                    deps.append(("e", f, i))
            for c in self.chans:
                if c.count:
                    deps.append(("d", c, c.count))
            self.ops[e].append(Op(None, deps))

    def emit(self):
        nc = self.nc
        LIM = 16000
        nsig = {}
        for e in ENGS:
            c = 0
            for o in self.ops[e]:
                if o.signal:
                    c += 1
                o.cnt = c
            nsig[e] = c
        sems = {}
        for e in ENGS:
            sems[e] = [self.sem[e]]
            for i in range(1, (nsig[e] + LIM - 1) // LIM):
                sems[e].append(self.stack.enter_context(nc.semaphore("s_%s_%d" % (e, i))))
        stats = {}
        with nc.Block() as block:
            def run(eng, e):
                seen = {}
                nw = 0
                for o in self.ops[eng]:
                    for t in o.deps:
                        if t[0] == "e":
                            key = t[1]
                            val = self.ops[t[1]][t[2]].cnt
                            if seen.get(key, 0) >= val:
                                continue
                            seen[key] = val
                            e.wait_ge(sems[t[1]][(val - 1) // LIM], (val - 1) % LIM + 1)
                        else:
                            key = id(t[1])
                            val = t[2]
                            if seen.get(key, 0) >= val:
                                continue
                            seen[key] = val
                            e.wait_ge(t[1].sem, val)
                        nw += 1
                    if o.fn is None:
                        continue
                    try:
                        ins = o.fn(e)
                    except Exception:
                        print("FAILED OP", eng, "src lines", o.src, flush=True)
                        raise
                    if o.dma is not None:
                        ins.then_inc(o.dma[0].sem, 16)
                    elif o.signal:
                        ins.then_inc(sems[eng][(o.cnt - 1) // LIM], 1)
                stats[eng] = (len(self.ops[eng]), nw, nsig[eng])

            @block.tensor
            def _(e):
                run("pe", e)

            @block.scalar
            def _(e):
                run("act", e)

            @block.vector
            def _(e):
                run("dve", e)

            @block.gpsimd
            def _(e):
                run("pool", e)

            @block.sync
            def _(e):
                run("sp", e)
        return stats


def build(S, layers):
    NT = S // T
    nc = bass.Bass("TRN2", target_bir_lowering=False)

    def din(name, shape):
        return nc.dram_tensor(name, shape, F32, kind="ExternalInput").ap()

    x_d = din("x", [S, D])
    mem_d = din("mem", [256, D])
    par_d = din("params", [256, 128])
    cst_d = din("consts", [128, 64])
    relx_t = nc.dram_tensor("relx", [2, 16, 768], F32, kind="ExternalInput")
    w_in_d = din("w_in_ab", [2, D, 2560])
    poolw_d = din("pool_w", [2, 4, 128, 128])
    w_out_d = din("w_out_ab", [2, D, D])
    w_qkv_d = din("w_qkv", [2, D, 3 * D])
    w_oatt_d = din("w_o_att", [2, D, D])
    w_mq_d = din("w_mem_q", [4, D, D])
    w_mkv_d = din("w_mem_kv", [4, D, 2 * D])
    w_mo_d = din("w_mem_o", [4, D, D])
    w_ff1_d = din("w_ff1", [4, D, 4 * D])
    w_ff2_d = din("w_ff2", [4, 4 * D, D])
    y_d = nc.dram_tensor("y", [S, D], F32, kind="ExternalOutput").ap()
    xs_d = nc.dram_tensor("xs", [8, 128, S], F32).ap()

    def sb(name, shape, dt=F32):
        return nc.alloc_sbuf_tensor(name, shape, dt)

    ident = sb("ident", [128, 128]); b_ident = Buf()
    onesb = sb("onesb", [128, 128], BF16); b_onesb = Buf()
    onesf = sb("onesf", [128, 512]); b_onesf = Buf()
    pv = sb("pv", [128, 256]); b_pv = Buf()
    lbv = sb("lbv", [128, 32]); b_lbv = Buf()
    cst = sb("cst", [128, 64]); b_cst = Buf()
    mask = sb("mask", [128, 64]); b_mask = Buf()
    memT = sb("memT", [128, 8, 256], BF16); b_memT = Buf()
    kmT = sb("kmT", [128, 8, 256], BF16); b_kmT = bufs(8)
    vm = sb("vm", [128, 2, 1024], BF16); b_vm = bufs(2)
    xTs = [sb("xTa", [128, 8, 512]), sb("xTb", [128, 8, 512])]; b_xTs = [bufs(8), bufs(8)]
    X = {"t": xTs[0], "b": b_xTs[0]}
    hT = sb("hT", [128, 8, 512], BF16); b_hT = bufs(8)
    sq = sb("sq", [128, 8, 512], BF16); b_sq = bufs(8)
    yT = sb("yT", [128, 8, 512]); b_yT = bufs(8)
    rstd = sb("rstd", [128, 512]); b_rstd = Buf()
    big = sb("big", [128, 32, 512], BF16); b_big = bufs(32)
    pT2 = sb("pT2", [128, 2, 640], BF16); b_pT2 = bufs(2)
    wsl = [sb("wsl%d" % i, [128, 8, 512], BF16) for i in range(NW)]
    b_wsl = bufs(NW)
    MIXW = 14896
    mix = sb("mix", [128, MIXW])

    banks = [nc.alloc_psum_tensor("pb%d" % i, [128, 512], F32) for i in range(8)]
    b_bank = bufs(8)

    with ExitStack() as st:
        P = Prog(nc, st)
        c_w = [P.chan() for _ in range(NW)]
        c_x = P.chan(); c_xs = [P.chan(), P.chan()]; c_o = P.chan(); c_m1 = P.chan(); c_m2 = P.chan(); c_m3 = P.chan(); c_pw = P.chan(); c_e = [P.chan(), P.chan()]
        b_chain = Buf()

        wbi = [0]

        def wb():
            i = wbi[0] % 4
            wbi[0] += 1
            return banks[i], b_bank[i]

        def carve(off, n, dt=F32, **kw):
            v = mix[:, off:off + n]
            if dt == BF16:
                v = v.bitcast(BF16)
            return v

        def mm(out, lhsT, rhs, start, stop, reads, writes):
            P.op("pe", lambda e: e.matmul(out, lhsT=lhsT, rhs=rhs, start=start, stop=stop), reads, writes)

        def tp(out, in_, reads, writes):
            P.op("pe", lambda e: e.transpose(out, in_, ident[:]), [reads, b_ident], writes)

        def tpb(out, in_, reads, writes):
            P.op("pe", lambda e: e.transpose(out, in_, identb[:]), reads, writes)

        def act(out, in_, func, reads, writes, scale=1.0, bias=0.0):
            P.op("act", lambda e: e.activation(out=out, in_=in_, func=func, bias=bias, scale=scale), reads, writes)

        def tt(eng, out, in0, in1, op, reads, writes):
            P.op(eng, lambda e: e.tensor_tensor(out=out, in0=in0, in1=in1, op=op), reads, writes)

        def ts(eng, out, in0, s1, s2, op0, op1, reads, writes):
            if s2 is None:
                P.op(eng, lambda e: e.tensor_scalar(out=out, in0=in0, scalar1=s1, scalar2=None, op0=op0), reads, writes)
            else:
                P.op(eng, lambda e: e.tensor_scalar(out=out, in0=in0, scalar1=s1, scalar2=s2, op0=op0, op1=op1), reads, writes)

        def stt(out, in0, scalar, in1, op0, op1, reads, writes):
            P.op("dve", lambda e: e.scalar_tensor_tensor(out=out, in0=in0, scalar=scalar, in1=in1, op0=op0, op1=op1), reads, writes)

        def cp(eng, out, in_, reads, writes):
            if eng == "act":
                P.op("act", lambda e: e.copy(out=out, in_=in_), reads, writes)
            else:
                P.op(eng, lambda e: e.tensor_copy(out=out, in_=in_), reads, writes)

        def recip(out, in_, reads, writes):
            P.op("dve", lambda e: e.reciprocal(out=out, in_=in_), reads, writes)

        def memset(eng, ap, val, writes):
            P.op(eng, lambda e: e.memset(ap, val), None, writes)

        def wpiece(w2d, c0):
            return w2d.rearrange("(kc p) n -> p kc n", p=128)[:, :, c0:c0 + 512]

        seq = []
        for l in layers:
            ei = l // 2
            for j in range(4):
                seq.append(("kv%d" % j, wpiece(w_mkv_d[l], 512 * j)))
            for t in range(NT):
                if l % 2 == 0 and "mix" in SUBS:
                    for j in (0, 1, 3, 4, 2):
                        seq.append(("win%d" % j, wpiece(w_in_d[ei], 512 * j)))
                    for j in range(2):
                        seq.append(("wout%d" % j, wpiece(w_out_d[ei], 512 * j)))
                elif "mix" in SUBS:
                    for j in range(6):
                        seq.append(("qkv%d" % j, wpiece(w_qkv_d[ei], 512 * j)))
                    for j in range(2):
                        seq.append(("wo%d" % j, wpiece(w_oatt_d[ei], 512 * j)))
                if "mem" in SUBS:
                    for j in range(2):
                        seq.append(("mq%d" % j, wpiece(w_mq_d[l], 512 * j)))
                    for j in range(2):
                        seq.append(("mo%d" % j, wpiece(w_mo_d[l], 512 * j)))
                if "mlp" in SUBS:
                    for j in range(8):
                        seq.append(("w1_%d" % j, wpiece(w_ff1_d[l], 512 * j)))
                    for h in range(2):
                        for r in range(4):
                            seq.append(("w2_%d_%d" % (h, r), wpiece(w_ff2_d[l][1024 * r:1024 * (r + 1), :], 512 * h)))
        rs = {"issued": 0, "used": 0}

        def ring_get(name):
            i = rs["used"]
            assert seq[i][0] == name, (seq[i][0], name)
            while rs["issued"] < min(i + NW, len(seq)):
                k = rs["issued"]
                P.dma("pool", c_w[k % NW], wsl[k % NW][:], seq[k][1], writes=[b_wsl[k % NW]])
                rs["issued"] += 1
            rs["used"] += 1
            return wsl[i % NW], b_wsl[i % NW]

        memset("pool", ident[:], 0.0, [b_ident])
        P.op("pool", lambda e: e.affine_select(out=ident[:], in_=ident[:], pattern=[[1, 128]], compare_op=ALU.not_equal,
                                               fill=1.0, base=0, channel_multiplier=-1), None, [b_ident])
        memset("pool", onesb[:], 1.0, [b_onesb])
        memset("pool", onesf[:], 1.0, [b_onesf])
        memset("pool", mask[:], 1.0, [b_mask])
        for hh in range(2):
            P.op("pool", lambda e, hh=hh: e.affine_select(out=mask[64 * hh:64 * hh + 64, :], in_=mask[64 * hh:64 * hh + 64, :],
                                                        pattern=[[1, 64]], compare_op=ALU.is_ge, fill=0.0, base=0,
                                                        channel_multiplier=-1), None, [b_mask])
        ptmp = carve(0, 256).rearrange("p (r c) -> p r c", c=128); b_ptmp = Buf()
        P.dma("sp", c_m1, ptmp, par_d.rearrange("(r p) c -> p r c", p=128), writes=[b_ptmp])
        P.dma("sp", c_m2, cst[:], cst_d, writes=[b_cst])
        bk, bb = wb()
        for r in range(2):
            tp(bk[:, r * 128:(r + 1) * 128], ptmp[:, r, :], [b_ptmp], [bb])
        cp("dve", pv[:], bk[:, 0:256], [bb], [b_pv])
        tmp = carve(256, 64); b_tmp = Buf()
        tt("dve", tmp[:, 0:4], pv[:, 212:216], pv[:, 208:212], ALU.subtract, [b_pv], [b_tmp])
        act(tmp[:, 4:8], tmp[:, 0:4], AF.Exp, [b_tmp], [b_tmp])
        act(tmp[:, 8:12], tmp[:, 0:4], AF.Exp, [b_tmp], [b_tmp], scale=-1.0)
        ts("dve", tmp[:, 4:12], tmp[:, 4:12], 1.0, None, ALU.add, None, [b_tmp], [b_tmp])
        recip(tmp[:, 4:12], tmp[:, 4:12], [b_tmp], [b_tmp])
        tt("dve", lbv[:, 0:4], tmp[:, 4:8], tmp[:, 4:8], ALU.subtract, [b_tmp], [b_lbv])
        tt("dve", tmp[:, 12:16], tmp[:, 4:8], tmp[:, 8:12], ALU.add, [b_tmp], [b_tmp])
        tt("dve", lbv[:, 4:8], tmp[:, 12:16], tmp[:, 4:8], ALU.subtract, [b_tmp], [b_lbv])
        ts("dve", lbv[:, 0:8], lbv[:, 0:8], 0.0, 1.0, ALU.max, ALU.min, [b_lbv], [b_lbv])
        ts("dve", lbv[:, 8:16], lbv[:, 0:8], -1.0, 1.0, ALU.mult, ALU.add, [b_lbv], [b_lbv])
        memf = carve(512, 2048).rearrange("p (m f) -> p m f", f=1024); b_memf = Buf()
        P.dma("sp", c_m3, memf, mem_d.rearrange("(m p) f -> p m f", p=128), writes=[b_memf])
        junk = big[:, 0:2, :].rearrange("p a b -> p (a b)")
        for mb in range(2):
            P.op("act", lambda e, mb=mb: e.activation(out=junk, in_=memf[:, mb, :], func=AF.Square,
                                                      accum_out=tmp[:, 16 + mb:17 + mb]), [b_memf], [b_big[0], b_big[1], b_tmp])
        act(tmp[:, 18:20], tmp[:, 16:18], AF.Ln, [b_tmp], [b_tmp], scale=1.0 / D, bias=EPS)
        act(tmp[:, 20:22], tmp[:, 18:20], AF.Exp, [b_tmp], [b_tmp], scale=-0.5)
        for mb in range(2):
            ts("dve", memf[:, mb, :], memf[:, mb, :], tmp[:, 20 + mb:21 + mb], None, ALU.mult, None, [b_memf, b_tmp], [b_memf])
        for c in range(8):
            bk, bb = wb()
            for mb in range(2):
                tp(bk[:, mb * 128:(mb + 1) * 128], memf[:, mb, c * 128:(c + 1) * 128], [b_memf], [bb])
            ts("dve", memT[:, c, :], bk[:, 0:256], pv[:, 192 + c:193 + c], None, ALU.mult, None, [bb, b_pv], [b_memT])

        def rstd_from(bk, bb, dim):
            act(rstd[:], bk[:], AF.Ln, [bb], [b_rstd], scale=1.0 / dim, bias=EPS)
            act(rstd[:], rstd[:], AF.Exp, [b_rstd], [b_rstd], scale=-0.5)

        def prenorm(l, j):
            for c in range(8):
                act(sq[:, c, :], X["t"][:, c, :], AF.Square, [X["b"][c]], [b_sq[c]])
            bk, bb = wb()
            for c in range(8):
                mm(bk[:], onesb[:], sq[:, c, :], c == 0, c == 7, [b_onesb, b_sq[c]], [bb])
            rstd_from(bk, bb, D)
            for c in range(8):
                g = pv[:, (l * 6 + j) * 8 + c:(l * 6 + j) * 8 + c + 1]
                stt(hT[:, c, :], X["t"][:, c, :], g, rstd[:], ALU.mult, ALU.mult, [X["b"][c], b_pv, b_rstd], [b_hT[c]])

        def post_evac(c, bk, bb):
            cp("act", yT[:, c, :], bk[:], [bb], [b_yT[c]])
            act(sq[:, c, :], bk[:], AF.Square, [bb], [b_sq[c]])

        def post_finish(l, j):
            bk, bb = wb()
            for c in range(8):
                mm(bk[:], onesb[:], sq[:, c, :], c == 0, c == 7, [b_onesb, b_sq[c]], [bb])
            rstd_from(bk, bb, D)
            for c in range(8):
                g = pv[:, (l * 6 + j) * 8 + c:(l * 6 + j) * 8 + c + 1]
                tt("dve", yT[:, c, :], yT[:, c, :], rstd[:], ALU.mult, [b_yT[c], b_rstd], [b_yT[c]])
                stt(X["t"][:, c, :], yT[:, c, :], g, X["t"][:, c, :], ALU.mult, ALU.add, [b_yT[c], b_pv, X["b"][c]], [X["b"][c]])

        def proj_fm(names, nout, src, b_src, evac):
            for oc in range(nout):
                if oc % 4 == 0:
                    pc, pb = ring_get(names[oc // 4])
                bk, bb = wb()
                for kc in range(8):
                    mm(bk[:], pc[:, kc, (oc % 4) * 128:(oc % 4 + 1) * 128], src[:, kc, :], kc == 0, kc == 7,
                       [pb, b_src[kc]], [bb])
                evac(oc, bk, bb)

        def proj_tm(name, src, b_src, evac):
            pc, pb = ring_get(name)
            for tb in range(4):
                bk, bb = wb()
                for kc in range(8):
                    mm(bk[:], src[:, kc, tb * 128:(tb + 1) * 128], pc[:, kc, :], kc == 0, kc == 7, [pb, b_src[kc]], [bb])
                evac(tb, bk, bb)

        ytok = yT[:].rearrange("p (tb h) f -> p tb (h f)", h=2)

        def load_x(l, t, first, par):
            if first:
                src = x_d[t * T:(t + 1) * T, :].rearrange("(tb p) f -> p tb f", p=128)
                P.dma("sp", c_x, ytok, src, writes=b_yT)
                for c in range(8):
                    bk, bb = wb()
                    for tb in range(4):
                        tp(bk[:, tb * 128:(tb + 1) * 128], ytok[:, tb, c * 128:(c + 1) * 128], [b_yT[2 * tb], b_yT[2 * tb + 1]], [bb])
                    cp("act" if c % 2 else "dve", xTs[par][:, c, :], bk[:], [bb], [b_xTs[par][c]])
            else:
                P.dma("sp", c_xs[par], xTs[par][:], xs_d[:, :, t * T:(t + 1) * T].rearrange("c p s -> p c s"), reads=[b_xs[t]], writes=b_xTs[par])

        def store_x(l, t, last):
            if last:
                for tb in range(4):
                    for h in range(2):
                        bk, bb = wb()
                        for cc in range(4):
                            c = 4 * h + cc
                            tp(bk[:, cc * 128:(cc + 1) * 128], X["t"][:, c, tb * 128:(tb + 1) * 128], [X["b"][c]], [bb])
                        cp("act" if h else "dve", ytok[:, tb, h * 512:(h + 1) * 512], bk[:], [bb], [b_yT[2 * tb + h]])
                dst = y_d[t * T:(t + 1) * T, :].rearrange("(tb p) f -> p tb f", p=128)
                return P.dma("sp", c_o, dst, ytok, reads=b_yT, writes=[b_chain])
            else:
                return P.dma("sp", c_o, xs_d[:, :, t * T:(t + 1) * T].rearrange("c p s -> p c s"), X["t"][:], reads=X["b"], writes=[b_xs[t], b_chain])

        b_xs = bufs(NT)

        def mem_attn(l):
            prenorm(l, 2)
            qT = big[:, 0:8, :]; b_q = b_big[0:8]
            oT = big[:, 8:16, :]; b_o = b_big[8:16]
            proj_fm(["mq0", "mq1"], 8, hT, b_hT,
                    lambda oc, bk, bb: cp("act" if oc % 2 else "dve", qT[:, oc, :], bk[:], [bb], [b_q[oc]]))
            for hd in range(4):
                pts = []
                for mb in range(2):
                    bk, bb = wb()
                    for i in range(2):
                        c = 2 * hd + i
                        mm(bk[:], kmT[:, c, mb * 128:(mb + 1) * 128], qT[:, c, :], i == 0, i == 1, [b_kmT[c], b_q[c]], [bb])
                    pt = big[:, 16 + mb, :]; bp = b_big[16 + mb]
                    act(pt, bk[:], AF.Exp, [bb], [bp], scale=1.0 / 16.0)
                    pts.append((pt, bp))
                dk, db = wb()
                for mb in range(2):
                    mm(dk[:], onesb[:], pts[mb][0], mb == 0, mb == 1, [b_onesb, pts[mb][1]], [db])
                rd = big[:, 18:20, :].rearrange("p a b -> p (a b)").bitcast(F32); brd = [b_big[18], b_big[19]]
                recip(rd, dk[:], [db], brd)
                for dc in range(2):
                    c = 2 * hd + dc
                    bk, bb = wb()
                    for mb in range(2):
                        mm(bk[:], vm[:, mb, c * 128:(c + 1) * 128], pts[mb][0], mb == 0, mb == 1, [b_vm[mb], pts[mb][1]], [bb])
                    tt("dve", oT[:, c, :], bk[:], rd, ALU.mult, [bb, brd], [b_o[c]])
            proj_fm(["mo0", "mo1"], 8, oT, b_o, post_evac)
            post_finish(l, 3)

        def mlp(l):
            prenorm(l, 4)

            def ev1(hc, bk, bb):
                act(big[:, hc, :], bk[:], AF.Relu, [bb], [b_big[hc]])
                tt("dve", big[:, hc, :], big[:, hc, :], big[:, hc, :], ALU.mult, [b_big[hc]], [b_big[hc]])
            proj_fm(["w1_%d" % j for j in range(8)], 32, hT, b_hT, ev1)
            for h in range(2):
                ab = (4, 5, 6, 7) if h == 0 else (0, 1, 2, 3)
                for r in range(4):
                    pc, pb = ring_get("w2_%d_%d" % (h, r))
                    for i in range(4):
                        for kk in range(8):
                            hc = 8 * r + kk
                            mm(banks[ab[i]][:], pc[:, kk, i * 128:(i + 1) * 128], big[:, hc, :],
                               r == 0 and kk == 0, r == 3 and kk == 7, [pb, b_big[hc]], [b_bank[ab[i]]])
                for i in range(4):
                    post_evac(4 * h + i, banks[ab[i]], b_bank[ab[i]])
            post_finish(l, 5)

        o = [0]

        def take(n):
            a = o[0]
            o[0] += n
            return a
        E_UB = take(4 * 528); E_QS = take(1024); E_KS = take(1024); E_SG = take(1024); E_VT = take(1024)
        E_TMP = [take(512) for _ in range(7)]
        E_KT = [take(128) for _ in range(2)]
        E_G = take(1024); E_SB = take(512); E_S = take(512); E_AM = [take(256) for _ in range(2)]
        E_PT = [take(528) for _ in range(3)]; E_FV = take(64); E_PW = take(256); E_FVH = take(96)
        assert o[0] <= MIXW, o[0]
        ub = carve(E_UB, 4 * 528).rearrange("p (g t) -> p g t", t=528); b_ub = bufs(4)
        qs = carve(E_QS, 1024, BF16).rearrange("p (h t) -> p h t", t=512); b_qs = bufs(4)
        ks = carve(E_KS, 1024, BF16).rearrange("p (h t) -> p h t", t=512); b_ks = bufs(4)
        sg = carve(E_SG, 1024, BF16).rearrange("p (h t) -> p h t", t=512); b_sg = bufs(4)
        vt = carve(E_VT, 1024, BF16).rearrange("p (b f) -> p b f", f=512); b_vt = bufs(4)
        tmpf = [carve(a, 512) for a in E_TMP]; b_tmpf = bufs(7)
        kst = [carve(a, 128, BF16).rearrange("p (b d) -> p b d", d=128) for a in E_KT]; b_kst = bufs(2)
        Gs = carve(E_G, 1024).rearrange("p (j e) -> p j e", e=128); b_Gs = Buf()
        Sb = carve(E_SB, 512, BF16).rearrange("p (j e) -> p j e", e=128); b_Sb = Buf()
        Sst = carve(E_S, 512).rearrange("p (h e) -> p h e", e=128); b_S = bufs(4)
        Am = [carve(a, 256, BF16) for a in E_AM]; b_Am = bufs(2)
        ptm = [carve(a, 528) for a in E_PT]; b_ptm = bufs(3)
        fv = carve(E_FV, 64); b_fv = Buf()
        poolw = carve(E_PW, 256, BF16).rearrange("p (g d) -> p g d", d=128); b_poolw = Buf()
        fvh = carve(E_FVH, 96).rearrange("p (h a) -> p h a", a=24); b_fvh = bufs(4)

        def even_layer_init(l):
            ei = l // 2
            if MIXSTOP < 0.1:
                return
            pwst = tmpf[0].rearrange("p (g d) -> p g d", d=128)
            P.dma("sp", c_pw, pwst, poolw_d[ei].rearrange("g c d -> c g d"), writes=[b_tmpf[0]])
            cp("dve", poolw, pwst, [b_tmpf[0]], [b_poolw])
            for hd in range(4):
                memset("dve", Sst[:, hd, :], 0.0, [b_S[hd]])
            for g in range(4):
                memset("dve", ub[:, g, 0:16], 0.0, [b_ub[g]])
            for i in range(2):
                memset("dve", Am[i], 0.0, [b_Am[i]])

        def even_mixer(l, t):
            ei = l // 2
            prenorm(l, 0)
            mixin = big[:, 0:8, :]; b_mi = b_big[0:8]
            def pool_elem(g):
                w = 2 << g
                cur = ub[:, g, :]; bcur = b_ub[g]
                lo = 0
                sh = 1
                k = 0
                while sh < w:
                    nxt = ptm[k % 2]; bn = b_ptm[k % 2]
                    lo2 = lo + sh
                    tt("dve", nxt[:, lo2:528], cur[:, lo2:528], cur[:, lo2 - sh:528 - sh], ALU.add, [bcur], [bn])
                    cur, bcur, lo, sh, k = nxt, bn, lo2, sh * 2, k + 1
                pl = big[:, 24 + g, :]
                stt(pl, cur[:, 16:528], 1.0 / w, ub[:, g, 16:528], ALU.mult, ALU.subtract, [bcur, b_ub[g]], [b_big[24 + g]])
                if t == 0:
                    fx = ptm[2][:, 0:16]
                    tt("dve", fx, cur[:, 16:32], cst[:, g * 16:(g + 1) * 16], ALU.mult, [bcur, b_cst], [b_ptm[2]])
                    tt("dve", pl[:, 0:16], fx, ub[:, g, 16:32], ALU.subtract, [b_ptm[2], b_ub[g]], [b_big[24 + g]])
                cp("dve", ub[:, g, 0:16], ub[:, g, 512:528], [b_ub[g]], [b_ub[g]])

            def pool_mm(g):
                bk, bb = wb()
                mm(bk[:], poolw[:, g, :], big[:, 24 + g, :], True, True, [b_poolw, b_big[24 + g]], [bb])
                P.op("act", lambda e: e.activation(out=mixin[:, g, :], in_=bk[:], func=AF.Copy, scale=pv[:, 200 + ei * 4 + g:201 + ei * 4 + g]),
                     [bb, b_pv], [b_mi[g]])

            def gate(th, name, f):
                if MIXSTOP >= th:
                    f()
                else:
                    ring_get(name)
            gate(0.2, "win0", lambda: proj_fm(["win0"], 4, hT, b_hT,
                    lambda oc, bk, bb: cp("act", ub[:, oc, 16:528], bk[:], [bb], [b_ub[oc]])))
            for g in range(4):
                pool_elem(g)
            gate(0.3, "win1", lambda: proj_fm(["win1"], 4, hT, b_hT,
                    lambda oc, bk, bb: cp("dve", qs[:, oc, :], bk[:], [bb], [b_qs[oc]])))

            def f32v(c0):
                return big[:, c0:c0 + 2, :].rearrange("p a b -> p (a b)").bitcast(F32), [b_big[c0], b_big[c0 + 1]]
            EB = [(tmpf[0], [b_tmpf[0]]), f32v(17), f32v(19), f32v(21)]
            TS = [[(tmpf[1], [b_tmpf[1]]), (tmpf[2], [b_tmpf[2]]), (tmpf[3], [b_tmpf[3]])],
                  [(tmpf[4], [b_tmpf[4]]), (tmpf[5], [b_tmpf[5]]), (tmpf[6], [b_tmpf[6]])]]
            FV = [(fv, [b_fv]), (big[:, 23, :].bitcast(F32), [b_big[23]])]

            def run_interleaved(gens):
                gens = list(gens)
                while gens:
                    for g_ in list(gens):
                        try:
                            next(g_)
                        except StopIteration:
                            gens.remove(g_)

            def ev_fz(hd, bk, bb):
                act(EB[hd][0], bk[:], AF.Sigmoid, [bb], EB[hd][1])

            def fz_chain(hd):
                par = hd % 2
                E_, bE = EB[hd]
                (f_, bf_), (k_, bk_), (ex_, bex) = TS[par]
                fvx, bfv = FV[par]
                lf_, blf = E_, bE
                B_, bB = f_, bf_
                bq_, bbq = E_, bE
                lb = lbv[:, ei * 4 + hd:ei * 4 + hd + 1]
                oml = lbv[:, 8 + ei * 4 + hd:8 + ei * 4 + hd + 1]
                P.op("act", lambda e: e.activation(out=f_, in_=E_, func=AF.Identity, scale=oml, bias=lb), [bE, b_lbv], [bf_])
                yield
                act(k_, f_, AF.Identity, [bf_], [bk_], scale=-1.0, bias=1.0)
                act(lf_, f_, AF.Ln, [bf_], [blf])
                yield
                P.op("dve", lambda e: e.tensor_tensor_scan(out=B_, data0=onesf[:], data1=lf_, initial=0.0,
                                                           op0=ALU.mult, op1=ALU.add), [b_onesf, blf], [bB])
                yield
                B3 = B_.rearrange("p (j s) -> p j s", s=64)
                cmid = B3[:, :, 31:32]
                bend = B3[:, :, 63:64]
                memset("dve", fvx[:, 24:25], 0.0, [bfv])
                cp("dve", fvx[:, 25:32].rearrange("p (j o) -> p j o", o=1), B3[:, 0:7, 63:64], [bB], [bfv])
                bs3 = fvx[:, 24:32].rearrange("p (j o) -> p j o", o=1)
                tt("dve", fvx[:, 0:8].rearrange("p (j o) -> p j o", o=1), cmid, bs3, ALU.subtract, [bB, bfv], [bfv])
                tt("dve", fvx[:, 8:16].rearrange("p (j o) -> p j o", o=1), bend, bs3, ALU.subtract, [bB, bfv], [bfv])
                tt("dve", fvx[:, 16:24].rearrange("p (j o) -> p j o", o=1), bend, cmid, ALU.subtract, [bB], [bfv])
                yield
                act(fvx[:, 32:56], fvx[:, 0:24], AF.Exp, [bfv], [bfv])
                tt("dve", bq_.rearrange("p (j s) -> p j s", s=64), B3, cmid.to_broadcast([128, 8, 64]), ALU.subtract, [bB], [bbq])
                yield
                act(ex_, bq_, AF.Exp, [bbq], [bex])
                cp("dve", fvh[:, hd, :], fvx[:, 32:56], [bfv], [b_fvh[hd]])
                yield
                tt("dve", qs[:, hd, :], qs[:, hd, :], ex_, ALU.mult, [b_qs[hd], bex], [b_qs[hd]])
                yield
                act(ex_, bq_, AF.Exp, [bbq], [bex], scale=-1.0)
                yield
                tt("dve", ks[:, hd, :], k_, ex_, ALU.mult, [bk_, bex], [b_ks[hd]])
                yield
            gate(0.5, "win3", lambda: proj_tm("win3", hT, b_hT, lambda tb, bk, bb: cp("act", vt[:, tb, :], bk[:], [bb], [b_vt[tb]])))

            def ev_g(hd, bk, bb):
                act(sg[:, hd, :], bk[:], AF.Silu, [bb], [b_sg[hd]])
            gate(1, "win4", lambda: proj_fm(["win4"], 4, hT, b_hT, ev_g))
            proj_fm(["win2"], 4, hT, b_hT, ev_fz)
            for g in range(4):
                pool_mm(g)
            run_interleaved([fz_chain(0), fz_chain(1)])
            run_interleaved([fz_chain(2), fz_chain(3)])

            kst2 = [big[:, 8, 0:256].rearrange("p (b d) -> p b d", d=128), big[:, 8, 256:512].rearrange("p (b d) -> p b d", d=128)]
            Gs2 = big[:, 9:13, :].rearrange("p a b -> p (a b)").bitcast(F32).rearrange("p (j e) -> p j e", e=128)
            Sb2 = big[:, 13:15, :].rearrange("p a b -> p (a b)").rearrange("p (j e) -> p j e", e=128)
            oc2, boc2 = f32v(15)
            RES = [dict(kst=kst, bkst=[[b_kst[0]], [b_kst[1]]], Gs=Gs, bGs=[b_Gs], Sb=Sb, bSb=[b_Sb], oc=tmpf[2], boc=[b_tmpf[2]], banks=(0, 1), bGj=bufs(8)),
                   dict(kst=kst2, bkst=[[b_big[8]], [b_big[8]]], Gs=Gs2, bGs=b_big[9:13], Sb=Sb2, bSb=b_big[13:15], oc=oc2, boc=boc2, banks=(2, 3), bGj=bufs(8))]

            def hgrn_head(hd):
                R_ = RES[hd % 2]
                cnt = [0]

                def hb():
                    i = R_["banks"][cnt[0] % 2]
                    cnt[0] += 1
                    return banks[i], b_bank[i]
                kst_, bkst_, Gs_, bGs_, Sb_, bSb_, oc_, boc = R_["kst"], R_["bkst"], R_["Gs"], R_["bGs"], R_["Sb"], R_["bSb"], R_["oc"], R_["boc"]
                F1 = fvh[:, hd, 0:8]; F2 = fvh[:, hd, 8:16]; F3 = fvh[:, hd, 16:24]
                ak, ab = hb()
                for j in range(8):
                    r0 = 64 * (j % 2)
                    mm(ak[r0:r0 + 64, j * 64:(j + 1) * 64], ks[:, hd, j * 64:(j + 1) * 64], qs[:, hd, j * 64:(j + 1) * 64], True, True,
                       [b_ks[hd], b_qs[hd]], [ab])
                am = Am[hd % 2]; bam = b_Am[hd % 2]
                for r in range(2):
                    r0 = 64 * r
                    src = ak[r0:r0 + 64, :].rearrange("p (j two s) -> p j two s", two=2, s=64)[:, :, r, :]
                    dst = am[r0:r0 + 64, :].rearrange("p (j two s) -> p j two s", two=2, s=64)[:, :, r, :]
                    tt("dve", dst, src, mask[r0:r0 + 64, :].rearrange("p (o s) -> p o s", o=1).to_broadcast([64, 4, 64]), ALU.mult,
                       [ab, b_mask], [bam])
                yield
                for half in range(2):
                    bk, bb = hb()
                    bkb = bk[:].bitcast(BF16)
                    for q in range(2):
                        blk = 2 * half + q
                        tpb(bkb[:, q * 128:(q + 1) * 128], ks[:, hd, blk * 128:(blk + 1) * 128], [b_ks[hd], b_identb], [bb])
                    cp("act", kst_[half].rearrange("p b d -> p (b d)"), bkb[:, 0:256], [bb], bkst_[half])
                    yield
                Gs4 = Gs_.rearrange("p (jj two) e -> p jj two e", two=2)
                for r in range(2):
                    r0 = 64 * r
                    bk, bb = hb()
                    for jj in range(4):
                        blk = jj
                        mm(bk[:, jj * 128:(jj + 1) * 128], kst_[blk // 2][r0:r0 + 64, blk % 2, :], vt[r0:r0 + 64, blk, hd * 128:(hd + 1) * 128],
                           True, True, [bkst_[blk // 2], b_vt[blk]], [bb])
                    f3 = F3.rearrange("p (jj two) -> p jj two", two=2)[:, :, r:r + 1]
                    tt("dve", Gs4[:, :, r, :], bk[:].rearrange("p (j e) -> p j e", e=128),
                       f3.to_broadcast([128, 4, 128]), ALU.mult, [bb, b_fvh[hd]], [bGs_, [R_["bGj"][2 * jj + r] for jj in range(4)]])
                    yield
                bGj = R_["bGj"]
                for j in range(8):
                    Sj = Sst[:, hd, :] if j == 0 else Gs_[:, j - 1, :]
                    bSj = [b_S[hd]] if j == 0 else [bGj[j - 1]]
                    extra = bGs_ if j == 7 else []
                    P.op("act", lambda e, j=j, Sj=Sj: e.activation(out=Sb_[:, j, :], in_=Sj, func=AF.Copy, scale=F1[:, j:j + 1]),
                         [bSj, b_fvh[hd], extra], bSb_)
                    stt(Gs_[:, j, :], Sj, F2[:, j:j + 1], Gs_[:, j, :], ALU.mult, ALU.add, [bSj, b_fvh[hd], bGj[j]], [bGj[j]])
                cp("dve", Sst[:, hd, :], Gs_[:, 7, :], [bGj[7], bGs_], [b_S[hd]])
                yield
                ok, ob = hb()
                for blk in range(4):
                    mm(ok[:, blk * 128:(blk + 1) * 128], vt[:, blk, hd * 128:(hd + 1) * 128], am[:, blk * 128:(blk + 1) * 128], True, False,
                       [b_vt[blk], bam], [ob])
                    for r in range(2):
                        j = 2 * blk + r
                        mm(ok[:, j * 64:(j + 1) * 64], Sb_[:, j, :], qs[:, hd, j * 64:(j + 1) * 64], False, r == 1, [bSb_, b_qs[hd]], [ob])
                osq = hT[:, 4 + hd, :]
                cp("act", oc_, ok[:], [ob], boc)
                yield
                act(osq, oc_, AF.Square, boc, [b_hT[4 + hd]])
                sk, sbb = hb()
                mm(sk[:], onesb[:], osq, True, True, [b_onesb, b_hT[4 + hd]], [sbb])
                yield
                rstd_from(sk, sbb, 128)
                stt(oc_, oc_, pv[:, 216 + ei:217 + ei], rstd[:], ALU.mult, ALU.mult, [boc, b_pv, b_rstd], boc)
                yield
                tt("dve", mixin[:, 4 + hd, :], oc_, sg[:, hd, :], ALU.mult, [boc, b_sg[hd]], [b_mi[4 + hd]])
                yield
            run_interleaved([hgrn_head(0), hgrn_head(1)])
            run_interleaved([hgrn_head(2), hgrn_head(3)])
            if MIXSTOP < 6:
                for c in range(8):
                    memset("dve", mixin[:, c, :], 0.0, [b_mi[c]])
            if MIXPART == "pool":
                for c in range(4, 8):
                    memset("dve", mixin[:, c, :], 0.0, [b_mi[c]])
            if MIXPART == "rec":
                for c in range(4):
                    memset("dve", mixin[:, c, :], 0.0, [b_mi[c]])
            proj_fm(["wout0", "wout1"], 8, mixin, b_mi, post_evac)
            post_finish(l, 1)

        antiJ = sb("antiJ", [128, 128]); b_antiJ = Buf()
        memset("pool", antiJ[:], 0.0, [b_antiJ])
        P.op("pool", lambda e: e.affine_select(out=antiJ[:], in_=antiJ[:], pattern=[[1, 128]], compare_op=ALU.not_equal,
                                               fill=1.0, base=-127, channel_multiplier=1), None, [b_antiJ])
        identb = sb("identb", [128, 128], BF16); b_identb = Buf()
        cp("dve", identb[:], ident[:], [b_ident], [b_identb])

        O_KT = 0; O_VT = 4096; O_E = 8192; O_BR = 8192 + 5120
        kT = carve(O_KT, 4096, BF16).rearrange("p (s c t) -> p s c t", s=2, c=8); b_kT = [bufs(8), bufs(8)]
        vtk = carve(O_VT, 4096, BF16).rearrange("p (s b f) -> p s b f", s=2, b=4); b_vtk = [bufs(4), bufs(4)]
        Et = carve(O_E, 5120, BF16).rearrange("p (h m) -> p h m", m=640); b_E = bufs(16)
        braw = [carve(O_BR + 640 * i, 640) for i in range(2)]; b_braw = bufs(2)
        assert O_BR + 1280 <= MIXW

        def odd_layer_init(l):
            oi = l // 2
            for hh in range(16):
                br = braw[hh % 2]; bbr = b_braw[hh % 2]
                src = bass.AP(relx_t, (oi * 16 + hh) * 768, [[1, 128], [1, 640]])
                P.dma("sp", c_e[hh % 2], br, src, writes=[bbr])
                bA, bAb = wb()
                mm(bA[:], antiJ[:], br[:, 0:512], True, True, [b_antiJ, bbr], [bAb])
                bB, bBb = wb()
                mm(bB[:, 0:128], antiJ[:], br[:, 512:640], True, True, [b_antiJ, bbr], [bBb])
                act(Et[:, hh, 0:512], bA[:], AF.Exp, [bAb], [b_E[hh]])
                act(Et[:, hh, 512:640], bB[:, 0:128], AF.Exp, [bBb], [b_E[hh]])
                memset("dve", Et[64:128, hh, 0:64], 0.0, [b_E[hh]])
                memset("dve", Et[0:64, hh, 576:640], 0.0, [b_E[hh]])

        def odd_mixer(l, t):
            slot = t % 2
            prenorm(l, 0)
            qT = big[:, 0:8, :]; b_q = b_big[0:8]
            oT = big[:, 8:16, :]; b_o = b_big[8:16]
            proj_fm(["qkv0", "qkv1"], 8, hT, b_hT,
                    lambda oc, bk, bb: cp("act" if oc % 2 else "dve", qT[:, oc, :], bk[:], [bb], [b_q[oc]]))
            proj_fm(["qkv2", "qkv3"], 8, hT, b_hT,
                    lambda oc, bk, bb: cp("act" if oc % 2 else "dve", kT[:, slot, oc, :], bk[:], [bb], [b_kT[slot][oc]]))
            for h in range(2):
                proj_tm("qkv%d" % (4 + h), hT, b_hT,
                        lambda tb, bk, bb, h=h: cp("act" if tb % 2 else "dve", vtk[:, slot, tb, h * 512:(h + 1) * 512], bk[:], [bb], [b_vtk[slot][tb]]))
            items = [(qb, hp, r) for qb in range(4) for hp in range(8) for r in range(2)]

            def emit_scores(it):
                qb, hp, r = it
                G = 4 * t + qb
                nk = min(G, 4) + 1
                r0 = 64 * r
                sA, sAb = wb()
                sB = sBb = None
                if nk == 5:
                    sB, sBb = wb()
                kinfo = []
                for k in range(nk):
                    g = G - k
                    gs = (g // 4) % 2; gl = g % 4
                    kinfo.append((gs, gl))
                    bk_, bb_ = (sA, sAb) if k < 4 else (sB, sBb)
                    col = (k % 4) * 128
                    mm(bk_[:, col:col + 128], kT[r0:r0 + 64, gs, hp, gl * 128:(gl + 1) * 128], qT[r0:r0 + 64, hp, qb * 128:(qb + 1) * 128],
                       True, True, [b_kT[gs][hp], b_q[hp]], [bb_])
                return (it, nk, sA, sAb, sB, sBb, kinfo)

            def emit_rest(ctx, hi):
                (qb, hp, r), nk, sA, sAb, sB, sBb, kinfo = ctx
                hh = 2 * hp + r
                r0 = 64 * r
                pt = pT2[:, hi % 2, :]; bp = b_pT2[hi % 2]
                na = min(nk, 4) * 128
                act(pt[:, 0:na], sA[:, 0:na], AF.Exp, [sAb], [bp], scale=0.125)
                if nk == 5:
                    act(pt[:, 512:640], sB[:, 0:128], AF.Exp, [sBb], [bp], scale=0.125)
                tt("dve", pt[:, 0:nk * 128], pt[:, 0:nk * 128], Et[:, hh, 0:nk * 128], ALU.mult, [bp, b_E[hh]], [bp])
                po = banks[4 + hp // 4]; pob = b_bank[4 + hp // 4]
                pd = banks[6 + hp // 4]; pdb = b_bank[6 + hp // 4]
                cs = (hp % 4) * 128
                for k in range(nk):
                    gs, gl = kinfo[k]
                    mm(po[r0:r0 + 64, cs:cs + 128], vtk[:, gs, gl, hh * 64:(hh + 1) * 64], pt[:, k * 128:(k + 1) * 128], k == 0, k == nk - 1,
                       [b_vtk[gs][gl], bp], [pob])
                for k in range(nk):
                    mm(pd[r0:r0 + 64, cs:cs + 128], onesb[:, 0:64], pt[:, k * 128:(k + 1) * 128], k == 0, k == nk - 1,
                       [b_onesb, bp], [pdb])
                if hp == 7 and r == 1:
                    for h in range(2):
                        rd = rstd[:]
                        recip(rd, banks[6 + h][:], [b_bank[6 + h]], [b_rstd])
                        tt("dve", oT[:, 4 * h:4 * h + 4, qb * 128:(qb + 1) * 128], banks[4 + h][:].rearrange("p (c q) -> p c q", q=128),
                           rd.rearrange("p (c q) -> p c q", q=128), ALU.mult, [b_bank[4 + h], b_rstd], b_o[4 * h:4 * h + 4])

            ctx = emit_scores(items[0])
            for i in range(len(items)):
                nxt = emit_scores(items[i + 1]) if i + 1 < len(items) else None
                emit_rest(ctx, i)
                ctx = nxt
            proj_fm(["wo0", "wo1"], 8, oT, b_o, post_evac)
            post_finish(l, 1)

        out_toks = []
        loaded = set()
        for li, l in enumerate(layers):
            first = li == 0
            last = li == len(layers) - 1
            P.barrier()
            for oc in range(8):
                if oc % 4 == 0:
                    pc, pb = ring_get("kv%d" % (oc // 4))
                bk, bb = wb()
                for kc in range(8):
                    mm(bk[:, 0:256], pc[:, kc, (oc % 4) * 128:(oc % 4 + 1) * 128], memT[:, kc, :], kc == 0, kc == 7, [pb, b_memT], [bb])
                cp("act", kmT[:, oc, :], bk[:, 0:256], [bb], [b_kmT[oc]])
            for h in range(2):
                pc, pb = ring_get("kv%d" % (2 + h))
                for mb in range(2):
                    bk, bb = wb()
                    for kc in range(8):
                        mm(bk[:], memT[:, kc, mb * 128:(mb + 1) * 128], pc[:, kc, :], kc == 0, kc == 7, [pb, b_memT], [bb])
                    cp("dve", vm[:, mb, h * 512:(h + 1) * 512], bk[:], [bb], [b_vm[mb]])
            if l % 2 == 0:
                even_layer_init(l)
            else:
                odd_layer_init(l)
            for t in range(NT):
                g = li * NT + t
                par = g % 2
                if g not in loaded:
                    load_x(l, t, first, par)
                    loaded.add(g)
                if g + 1 < len(layers) * NT and (g + 1) // NT > 0 and (g + 1) not in loaded:
                    l2 = layers[(g + 1) // NT]; t2 = (g + 1) % NT
                    if (g + 1) // NT == li or t2 == 0 and False:
                        load_x(l2, t2, False, 1 - par)
                        loaded.add(g + 1)
                X["t"] = xTs[par]; X["b"] = b_xTs[par]
                if "mix" in SUBS:
                    if l % 2 == 0:
                        even_mixer(l, t)
                    else:
                        odd_mixer(l, t)
                if "mem" in SUBS:
                    mem_attn(l)
                if "mlp" in SUBS:
                    mlp(l)
                tok = store_x(l, t, last)
                if last:
                    out_toks.append(tok)
        P.ops["sp"].append(Op(None, [out_toks[-1]]))
        stats = P.emit()
    return nc, stats


def _host_prep(inputs):
    f = lambda a: np.ascontiguousarray(np.asarray(a, dtype=np.float32))
    ng = f(inputs["norm_gains"]).reshape(192, 128)
    params = np.concatenate([
        ng,
        f(inputs["mem_norm_gain"]).reshape(8, 128),
        f(inputs["pool_scale"]).reshape(8, 128),
        f(inputs["rec_lb_logits"]).reshape(8, 128),
        f(inputs["rec_out_gain"]).reshape(2, 128),
        np.zeros((38, 128), np.float32)], axis=0)
    consts = np.zeros((128, 64), np.float32)
    for g, w in enumerate((2, 4, 8, 16)):
        consts[:, g * 16:(g + 1) * 16] = 1.0 / np.minimum(np.arange(1, 17), w)
    idx = np.clip(np.arange(768) - 127, -256, 256) + 256
    relx = f(inputs["rel_bias"])[:, :, idx]
    shared = {"params": params, "consts": consts, "relx": np.ascontiguousarray(relx)}
    for k in ("w_in_ab", "pool_w", "w_out_ab", "w_qkv", "w_o_att", "w_mem_q", "w_mem_kv", "w_mem_o", "w_ff1", "w_ff2"):
        shared[k] = f(inputs[k])
    return shared


_CACHE = {}


def kernel(**inputs):
    x = np.asarray(inputs["x"], dtype=np.float32)
    mem = np.asarray(inputs["mem"], dtype=np.float32)
    B, S, _ = x.shape
    shared = _host_prep(inputs)
    key = (S,)
    if key not in _CACHE:
        _CACHE[key] = build(S, [0, 1, 2, 3])[0]
    nc = _CACHE[key]
    in_maps = []
    for b in range(B):
        m = dict(shared)
        m["x"] = np.ascontiguousarray(x[b])
        m["mem"] = np.ascontiguousarray(mem[b])
        in_maps.append(m)
    res = run_bass_kernel_spmd(nc, in_maps, core_ids=list(range(B)))
    return np.stack([res.results[b]["y"] for b in range(B)], axis=0).astype(np.float32)
```
